# Optimizing a Trainium2 kernel written in Bass

```python
import math
import jax, jax.numpy as jnp
from jax import lax
import numpy as np


D_MODEL = 1024
BATCH = 8
SEQ = 2048
DEPTH = 2

HEAD_DIM = 64
DIL_CONFIGS = ((128, 1), (512, 4), (2048, 16))
N_DIL_GROUPS = len(DIL_CONFIGS)
HEADS_PER_GROUP = 4
N_DIL_HEADS = N_DIL_GROUPS * HEADS_PER_GROUP
N_DIFF_HEADS = 4
DIFF_V_DIM = 2 * HEAD_DIM
N_ALIBI_HEADS = N_DIL_HEADS + N_DIFF_HEADS
D_FF = ((8 * D_MODEL // 3 + 127) // 128) * 128
BLOCK = 128
EPS = 1e-6

DIL_WIDTH = N_DIL_HEADS * HEAD_DIM
DIL_OUT = HEADS_PER_GROUP * HEAD_DIM
DIFF_QK_WIDTH = N_DIFF_HEADS * 2 * HEAD_DIM
DIFF_V_WIDTH = N_DIFF_HEADS * DIFF_V_DIM
IN_WIDTHS = (DIL_WIDTH, DIL_WIDTH, DIL_WIDTH, DIFF_QK_WIDTH, DIFF_QK_WIDTH, DIFF_V_WIDTH, D_MODEL, D_MODEL)
IN_COLS = sum(IN_WIDTHS)
SPLITS = tuple(int(c) for c in np.cumsum(IN_WIDTHS)[:-1])

kernel_name = 'hybrid_dilated_diff_attn_macaron'


def rms_norm(x, g):
    xf = x.astype(jnp.float32)
    y = xf * lax.rsqrt(jnp.mean(xf * xf, axis=-1, keepdims=True) + EPS)
    return (y * g.astype(jnp.float32)).astype(x.dtype)


def alibi_slopes(n):
    return jnp.exp2(-8.0 * jnp.arange(1, n + 1, dtype=jnp.float32) / n)


def swiglu_ffn(x, norm_g, w_in, w_out):
    h = rms_norm(x, norm_g)
    gate, up = jnp.split(h @ w_in, 2, axis=-1)
    return (jax.nn.silu(gate) * up) @ w_out


def dilated_window_attention(q, k, v, slopes, window, dilation):
    b, s, h, d = q.shape
    steps = window // dilation
    assert steps <= BLOCK
    span = dilation * BLOCK
    s_pad = -(-s // span) * span
    n_sub = s_pad // dilation
    n_blk = n_sub // BLOCK

    def to_blocks(t):
        t = jnp.pad(t, ((0, 0), (0, s_pad - s), (0, 0), (0, 0)))
        t = t.reshape(b, n_sub, dilation, h, d).transpose(0, 2, 1, 3, 4)
        return t.reshape(b, dilation, n_blk, BLOCK, h, d)

    def with_prev(t):
        prev = jnp.pad(t, ((0, 0), (0, 0), (1, 0), (0, 0), (0, 0), (0, 0)))[:, :, :-1]
        return jnp.concatenate([prev, t], axis=3)

    qb = to_blocks(q)
    kw = with_prev(to_blocks(k))
    vw = with_prev(to_blocks(v))
    logits = jnp.einsum('brnqhd,brnkhd->brnhqk', qb, kw).astype(jnp.float32)
    qi = jnp.arange(BLOCK)[:, None]
    ki = jnp.arange(2 * BLOCK)[None, :]
    dist = BLOCK + qi - ki
    key_sub = (jnp.arange(n_blk)[:, None, None] - 1) * BLOCK + ki[None]
    valid = (dist >= 0)[None] & (dist <= steps)[None] & (key_sub >= 0)
    bias = -slopes[:, None, None] * (dilation * dist).astype(jnp.float32)
    logits = jnp.where(valid[None, None, :, None], logits + bias, -jnp.inf)
    m = jnp.max(logits, axis=-1, keepdims=True)
    p = jnp.exp(logits - m)
    denom = jnp.sum(p, axis=-1, keepdims=True)
    out = jnp.einsum('brnhqk,brnkhd->brnqhd', p, vw.astype(jnp.float32))
    out = out / jnp.swapaxes(denom, 3, 4)
    lse = jnp.swapaxes((m + jnp.log(denom))[..., 0], 3, 4)
    out = out.reshape(b, dilation, n_sub, h, d).transpose(0, 2, 1, 3, 4).reshape(b, s_pad, h, d)[:, :s]
    lse = lse.reshape(b, dilation, n_sub, h).transpose(0, 2, 1, 3).reshape(b, s_pad, h)[:, :s]
    return out, lse


def differential_attention(q, k, v, slopes, lam):
    b, s, h, _, d = q.shape
    n_blk = s // BLOCK
    qb = q.reshape(b, n_blk, BLOCK, h, 2, d).transpose(1, 0, 2, 3, 4, 5)
    kpos = jnp.arange(s)
    vf = v.astype(jnp.float32)

    def one_block(args):
        q_blk, i = args
        qpos = i * BLOCK + jnp.arange(BLOCK)
        logits = jnp.einsum('bqhmd,bkhmd->bhmqk', q_blk, k).astype(jnp.float32)
        dist = qpos[:, None] - kpos[None, :]
        logits = logits - slopes[:, None, None, None] * dist.astype(jnp.float32)
        logits = jnp.where(dist >= 0, logits, -jnp.inf)
        p = jax.nn.softmax(logits, axis=-1)
        a = p[:, :, 0] - lam * p[:, :, 1]
        return jnp.einsum('bhqk,bkhe->bqhe', a, vf)

    out = lax.map(one_block, (qb, jnp.arange(n_blk)))
    return out.transpose(1, 0, 2, 3, 4).reshape(b, s, h, v.shape[-1])


def mixer_block(x, layer_idx, slopes, mix_norm, w_in, qk_gain_dil, qk_gain_diff, lambda_q, lambda_k,
                diff_subnorm, w_branch_dil, w_branch_diff, w_out):
    b, s, _ = x.shape
    h = rms_norm(x, mix_norm)
    proj = h @ w_in
    qa, ka, va, qd, kd, vd, gate_a, gate_b = jnp.split(proj, SPLITS, axis=-1)

    qa = rms_norm(qa.reshape(b, s, N_DIL_GROUPS, HEADS_PER_GROUP, HEAD_DIM), qk_gain_dil[0]) * HEAD_DIM ** -0.5
    ka = rms_norm(ka.reshape(b, s, N_DIL_GROUPS, HEADS_PER_GROUP, HEAD_DIM), qk_gain_dil[1])
    va = va.reshape(b, s, N_DIL_GROUPS, HEADS_PER_GROUP, HEAD_DIM)
    outs, lses = [], []
    for g, (window, dilation) in enumerate(DIL_CONFIGS):
        o_g, lse_g = dilated_window_attention(qa[:, :, g], ka[:, :, g], va[:, :, g],
                                              slopes[g * HEADS_PER_GROUP:(g + 1) * HEADS_PER_GROUP],
                                              window, dilation)
        outs.append(o_g)
        lses.append(lse_g)
    o_stack = jnp.stack(outs, axis=2)
    alpha = jax.nn.softmax(jnp.stack(lses, axis=2), axis=2)
    o_dil = jnp.sum(alpha[..., None] * o_stack, axis=2).reshape(b, s, DIL_OUT).astype(x.dtype)

    qd = rms_norm(qd.reshape(b, s, N_DIFF_HEADS, 2, HEAD_DIM), qk_gain_diff[0]) * HEAD_DIM ** -0.5
    kd = rms_norm(kd.reshape(b, s, N_DIFF_HEADS, 2, HEAD_DIM), qk_gain_diff[1])
    vd = vd.reshape(b, s, N_DIFF_HEADS, DIFF_V_DIM)
    lam_init = 0.8 - 0.6 * math.exp(-0.3 * layer_idx)
    lq = lambda_q.astype(jnp.float32)
    lk = lambda_k.astype(jnp.float32)
    lam = jnp.exp(jnp.sum(lq[0] * lk[0])) - jnp.exp(jnp.sum(lq[1] * lk[1])) + lam_init
    o_diff = differential_attention(qd, kd, vd, slopes[N_DIL_HEADS:], lam)
    o_diff = (rms_norm(o_diff, diff_subnorm) * (1.0 - lam_init)).reshape(b, s, DIFF_V_WIDTH).astype(x.dtype)

    y = jax.nn.sigmoid(gate_a) * (o_dil @ w_branch_dil) + jax.nn.sigmoid(gate_b) * (o_diff @ w_branch_diff)
    return y @ w_out


def setup_inputs(seed: int = 0) -> dict:
    key = jax.random.key(seed)
    ks = jax.random.split(key, 17)
    L, D, F = DEPTH, D_MODEL, D_FF
    f32 = jnp.float32

    def w(k, shape, fan_in):
        return jax.random.normal(k, shape, f32) * fan_in ** -0.5

    def gain(k, shape):
        return 1.0 + 0.02 * jax.random.normal(k, shape, f32)

    return {
        'x': jax.random.normal(ks[0], (BATCH, SEQ, D), f32),
        'ffn1_norm': gain(ks[1], (L, D)),
        'ffn1_w_in': w(ks[2], (L, D, 2 * F), D),
        'ffn1_w_out': w(ks[3], (L, F, D), F),
        'mix_norm': gain(ks[4], (L, D)),
        'w_in': w(ks[5], (L, D, IN_COLS), D),
        'qk_gain_dil': gain(ks[6], (L, 2, N_DIL_GROUPS, HEADS_PER_GROUP, HEAD_DIM)),
        'qk_gain_diff': gain(ks[7], (L, 2, N_DIFF_HEADS, 2, HEAD_DIM)),
        'lambda_q': 0.1 * jax.random.normal(ks[8], (L, 2, HEAD_DIM), f32),
        'lambda_k': 0.1 * jax.random.normal(ks[9], (L, 2, HEAD_DIM), f32),
        'diff_subnorm': gain(ks[10], (L, N_DIFF_HEADS, DIFF_V_DIM)),
        'w_branch_dil': w(ks[11], (L, DIL_OUT, D), DIL_OUT),
        'w_branch_diff': w(ks[12], (L, DIFF_V_WIDTH, D), DIFF_V_WIDTH),
        'w_out': w(ks[13], (L, D, D), D),
        'ffn2_norm': gain(ks[14], (L, D)),
        'ffn2_w_in': w(ks[15], (L, D, 2 * F), D),
        'ffn2_w_out': w(ks[16], (L, F, D), F),
    }


def reference(x, ffn1_norm, ffn1_w_in, ffn1_w_out, mix_norm, w_in, qk_gain_dil, qk_gain_diff,
              lambda_q, lambda_k, diff_subnorm, w_branch_dil, w_branch_diff, w_out,
              ffn2_norm, ffn2_w_in, ffn2_w_out):
    slopes = alibi_slopes(N_ALIBI_HEADS)
    for l in range(DEPTH):
        x = x + 0.5 * swiglu_ffn(x, ffn1_norm[l], ffn1_w_in[l], ffn1_w_out[l])
        x = x + mixer_block(x, l, slopes, mix_norm[l], w_in[l], qk_gain_dil[l], qk_gain_diff[l],
                            lambda_q[l], lambda_k[l], diff_subnorm[l], w_branch_dil[l],
                            w_branch_diff[l], w_out[l])
        x = x + 0.5 * swiglu_ffn(x, ffn2_norm[l], ffn2_w_in[l], ffn2_w_out[l])
    return x
```

```python
import math
from contextlib import ExitStack

import numpy as np
import concourse.bass as bass
import concourse.mybir as mybir
from concourse.bass_utils import run_bass_kernel_spmd

F32 = mybir.dt.float32
BF16 = mybir.dt.bfloat16
AF = mybir.ActivationFunctionType
ALU = mybir.AluOpType

D = 1024
S = 2048
L = 2
FF = 2816
NFC = 22
HD = 64
EPS = 1e-6
GROUPS = [list(range(0, 6)), list(range(6, 12)), list(range(12, 17)), list(range(17, 22))]
DILS = (1, 4, 16)


class Reg:
    __slots__ = ("name", "w", "r", "excl")

    def __init__(self, name, excl=False):
        self.name = name
        self.w = None
        self.r = []
        self.excl = excl


class Eng:
    def __init__(self, key):
        self.key = key
        self.count = 0
        self.pending = False
        self.seen = {}
        self.ops = []


class Prog:
    ENGS = ("pe", "act", "dve", "pool", "sp")

    def __init__(self):
        self.e = {k: Eng(k) for k in self.ENGS}
        self.dma_sems = {}

    def _collect(self, eng, reads, writes):
        need = {}

        def add(ev, same_ok):
            if ev is None:
                return
            k, v = ev
            if k == eng.key and not same_ok:
                return
            if need.get(k, 0) < v:
                need[k] = v

        for r in reads:
            add(r.w, True)
            if r.excl:
                for ev in r.r:
                    add(ev, False)
        same = eng.key != "pe"
        for w in writes:
            add(w.w, same)
            for ev in w.r:
                add(ev, same)
        waits = []
        for k, v in need.items():
            if eng.seen.get(k, 0) >= v:
                continue
            eng.seen[k] = v
            waits.append((k, v))
        return waits

    def op(self, engkey, fn, reads=(), writes=(), inc=True):
        eng = self.e[engkey]
        waits = self._collect(eng, reads, writes)
        ev = (engkey, eng.count + 1)
        if inc:
            eng.count += 1
            eng.pending = False
        else:
            eng.pending = True
        for r in reads:
            r.r.append(ev)
            if len(r.r) > 64:
                r.r = _compress(r.r)
        for w in writes:
            w.w = ev
            w.r = []
        eng.ops.append((waits, fn, inc, None))

    def dma(self, engkey, semname, fn, reads=(), writes=()):
        eng = self.e[engkey]
        waits = self._collect(eng, reads, writes)
        self.dma_sems[semname] = self.dma_sems.get(semname, 0) + 16
        ev = ("dma:" + semname, self.dma_sems[semname])
        for r in reads:
            r.r.append(ev)
        for w in writes:
            w.w = ev
            w.r = []
        eng.ops.append((waits, fn, False, semname))

    def wait_all(self, engkey, regs):
        eng = self.e[engkey]
        waits = self._collect(eng, regs, regs)
        eng.ops.append((waits, None, False, None))

    def emit(self, nc, stack):
        sems = {}
        for k in self.ENGS:
            sems[k] = stack.enter_context(nc.semaphore("s_" + k))
        for name in self.dma_sems:
            sems["dma:" + name] = stack.enter_context(nc.semaphore("d_" + name))
        for k in self.ENGS:
            assert not self.e[k].pending, k
        block = stack.enter_context(nc.Block())
        hooks = {"pe": block.tensor, "act": block.scalar, "dve": block.vector,
                 "pool": block.gpsimd, "sp": block.sync}

        def make(k):
            ops = self.e[k].ops

            def body(h):
                for waits, fn, inc, dsem in ops:
                    if fn is None:
                        for (sk, v) in waits:
                            h.wait_ge(sems[sk], v)
                        continue
                    for (sk, v) in waits[:-1]:
                        h.wait_ge(sems[sk], v)
                    ins = fn(h)
                    if waits:
                        ins._wait_ge(sems[waits[-1][0]], waits[-1][1])
                    if inc:
                        ins.then_inc(sems[k], 1)
                    if dsem is not None:
                        ins.then_inc(sems["dma:" + dsem], 16)
            return body

        for k in self.ENGS:
            hooks[k](make(k))


def _compress(evs):
    best = {}
    for k, v in evs:
        if best.get(k, 0) < v:
            best[k] = v
    return list(best.items())


class Arena:
    def __init__(self, tensor, ncols):
        self.t = tensor
        self.n = ncols
        self.live = []

    def region(self, name, off, ncols):
        assert off + ncols <= self.n, (name, off, ncols, self.n)
        reg = Reg(name)
        keep = []
        for (a, b, r) in self.live:
            if a < off + ncols and off < b:
                if r.w is not None:
                    reg.r.append(r.w)
                reg.r.extend(r.r)
            else:
                keep.append((a, b, r))
        reg.r = _compress(reg.r)
        keep.append((off, off + ncols, reg))
        self.live = keep
        return self.t[:, off:off + ncols], reg

    def multi(self, name, off, ncols, count):
        ap, parent = self.region(name, off, ncols)
        regs = []
        for i in range(count):
            r = Reg(f"{name}.{i}")
            r.r = list(parent.r)
            regs.append(r)
            self.live.append((off, off + ncols, r))
        return ap, regs


C_NORM = 0
C_QKDIL = C_NORM + 48
C_QKDIFF = C_QKDIL + 24
C_SUB = C_QKDIFF + 16
C_EPS = C_SUB + 8
C_LAMI = C_EPS + 1
C_OML = C_LAMI + 2
C_ALIBI = C_OML + 2
C_LQ = C_ALIBI + 76
C_LK = C_LQ + 256
NCST = C_LK + 256
C_NLAM = NCST
C_SUBS = C_NLAM + 2
NCST_ALL = C_SUBS + 8

NCB = 11 * 128


def _slopes():
    return [2.0 ** (-8.0 * (i + 1) / 16.0) for i in range(16)]


def build_nc(n_layers=L, stages=("ffn1", "mix", "ffn2"), mix_parts=("dil", "diff", "fin")):
    nc = bass.Bass("TRN2", target_bir_lowering=False)
    dt_in = lambda name, shape: nc.dram_tensor(name, shape, F32, kind="ExternalInput").ap()
    xT_d = dt_in("xT", [8, 128, S])
    f_wi = dt_in("f_wi", [L * 2 * NFC, 128, 2048])
    f_wo = dt_in("f_wo", [L * 2 * NFC, 128, 1024])
    m_wi = dt_in("m_wi", [L * 30, 128, 1024])
    m_fin = dt_in("m_fin", [L * 8, 128, 3072])
    m_wo = dt_in("m_wo", [L, 128, 8192])
    cst_d = dt_in("cst", [128, NCST])
    cb_d = dt_in("cb", [128, NCB])
    out_d = nc.dram_tensor("outT", [8, 128, S], F32, kind="ExternalOutput").ap()

    P = Prog()
    NAR = 16384 + 43008
    with ExitStack() as st:
        sb = lambda n, sh, dt: st.enter_context(nc.sbuf_tensor(n, sh, dt))
        xT = sb("xT_sb", [128, 8, S], F32)
        ar_t = sb("arena", [128, NAR], BF16)
        scr = sb("scr", [128, 8, 512], F32)
        cst = sb("cst_sb", [128, NCST_ALL], F32)
        cb = sb("cb_sb", [128, NCB], BF16)
        lamtmp = sb("lamtmp", [128, 8], F32)
        lamprod = sb("lamprod", [128, 256], F32)
        ps_all = st.enter_context(nc.psum_tensor("ps_all", [128, 8, 512], F32))
        banks = [ps_all[:, i, :] for i in range(8)]
        BR = [Reg(f"bank{i}", excl=True) for i in range(8)]
        AR = Arena(ar_t, NAR)

        XR = [[Reg(f"x{n}_{t}") for t in range(4)] for n in range(8)]
        SCR = [Reg(f"scr{i}") for i in range(8)]
        CST = Reg("cst")
        CB = Reg("cb")
        LAM = Reg("lam")
        OUT = Reg("out")

        ones_bf = cb[:, 0:128]
        bd_bf = cb[:, 128:256]
        maskv = cb[:, 256:256 + 1024].rearrange("p (m q) -> p m q", m=8)
        tri_bf = cb[:, 1280:1408]

        def cs(col):
            return cst[:, col:col + 1]

        def mm(out, lhsT, rhs, start, stop, reads, writes, inc, skip=False):
            if skip:
                P.op("pe", lambda e: e.matmul(out, lhsT=lhsT, rhs=rhs, start=start, stop=stop,
                                              skip_group_check=True), reads, writes, inc)
            else:
                P.op("pe", lambda e: e.matmul(out, lhsT=lhsT, rhs=rhs, start=start, stop=stop),
                     reads, writes, inc)

        def act(out, in_, func, reads, writes, bias=None, scale=None):
            kw = {}
            if bias is not None:
                kw["bias"] = bias
            if scale is not None:
                kw["scale"] = scale
            P.op("act", lambda e: e.activation(out=out, in_=in_, func=func, **kw), reads, writes)

        def tt(eng, out, in0, in1, op, reads, writes):
            P.op(eng, lambda e: e.tensor_tensor(out=out, in0=in0, in1=in1, op=op), reads, writes)

        def stt(eng, out, in0, scalar, in1, op0, op1, reads, writes):
            P.op(eng, lambda e: e.scalar_tensor_tensor(out=out, in0=in0, scalar=scalar, in1=in1,
                                                       op0=op0, op1=op1), reads, writes)

        def recip(out, in_, reads, writes):
            P.op("dve", lambda e: e.reciprocal(out=out, in_=in_), reads, writes)

        def rsqrt_act(dst, src, scale, reads, dreg):
            act(dst, src, AF.Ln, list(reads) + [CST], [dreg], bias=cs(C_EPS), scale=scale)
            act(dst, dst, AF.Exp, [dreg], [dreg], scale=-0.5)

        def recip_act(dst, src, reads, dreg):
            act(dst, src, AF.Ln, list(reads), [dreg])
            act(dst, dst, AF.Exp, [dreg], [dreg], scale=-1.0)

        def pipeline(units, depth=1):
            n = len(units)
            for i in range(min(depth, n)):
                units[i][0]()
            for i in range(n):
                if i + depth < n:
                    units[i + depth][0]()
                units[i][1]()

        def wdma(sem, out, in_, reads, writes):
            P.dma("pool", sem, lambda e: e.dma_start(out=out, in_=in_), reads, writes)

        P.dma("sp", "cst", lambda e: e.dma_start(out=cst[:, 0:NCST], in_=cst_d), writes=[CST])
        wdma("cb", cb[:], cb_d, [], [CB])
        xT_dv = xT_d.rearrange("n p s -> p n s")
        for t in range(4):
            for hf in range(2):
                P.dma("sp", f"x{t * 2 + hf}",
                      (lambda t, hf: lambda e: e.dma_start(out=xT[:, 4 * hf:4 * hf + 4, 512 * t:512 * t + 512],
                                                           in_=xT_dv[:, 4 * hf:4 * hf + 4, 512 * t:512 * t + 512]))(t, hf),
                      writes=[XR[n][t] for n in range(4 * hf, 4 * hf + 4)])

        hT_ap, _ = AR.region("hT", 0, 16384)
        hT = hT_ap.rearrange("p (k s) -> p k s", k=8)
        HR = [[Reg(f"h{k}_{t}") for t in range(4)] for k in range(8)]
        BASE = 16384

        def tsl(t):
            return slice(512 * t, 512 * t + 512)

        nsq_t = sb("nsq", [128, 2, 512], BF16)
        NSQR = [Reg("nsq0"), Reg("nsq1")]

        def norm_stats(gcol, t):
            sb_i = 6 + (t % 2)
            for k in range(8):
                i = k % 2
                act(nsq_t[:, i, :], xT[:, k, tsl(t)], AF.Square, [XR[k][t]], [NSQR[i]])
                mm(banks[sb_i][:], ones_bf, nsq_t[:, i, :], k == 0, k == 7, [CB, NSQR[i]], [BR[sb_i]], True)
            si = t % 2
            rsqrt_act(scr[:, si, :], banks[sb_i][:], 1.0 / D, [BR[sb_i]], SCR[si])

        def norm_write(gcol, t, k):
            si = t % 2
            stt("dve", hT[:, k, tsl(t)], xT[:, k, tsl(t)], cs(gcol + k), scr[:, si, :],
                ALU.mult, ALU.mult, [XR[k][t], SCR[si], CST], [HR[k][t]])

        def norm_tile(gcol, t):
            norm_stats(gcol, t)
            for k in range(8):
                norm_write(gcol, t, k)

        def rmsnorm(gcol):
            for t in range(4):
                norm_tile(gcol, t)

        def ffn(l, f, do_norm=True, next_g=None):
            if do_norm:
                rmsnorm(C_NORM + (l * 3 + (0 if f == 0 else 2)) * 8)
            tbase = (l * 2 + f) * NFC
            act_ap, act_rs = AR.multi("act", BASE, 12288, 24)
            actv = act_ap.rearrange("p (c s) -> p c s", c=6)
            ACTR = [[act_rs[c * 4 + t] for t in range(4)] for c in range(6)]
            win, winr = [], []
            for i in range(3):
                a, r = AR.region(f"win{i}", BASE + 28672 + 2048 * i, 2048)
                win.append(a.rearrange("p (k c) -> p k c", k=8))
                winr.append(r)
            wout, woutr = [], []
            for i in range(12):
                a, r = AR.region(f"wout{i}", BASE + 12288 + 1024 * i, 1024)
                wout.append(a)
                woutr.append(r)

            x0 = [XR[n][0] for n in range(8)] if (l == 0 and f == 0) else []

            def load_in(c):
                wdma(f"win{c % 3}", win[c % 3], f_wi[tbase + c].rearrange("p (k c) -> p k c", k=8),
                     x0 if c in (1, 2) else [], [winr[c % 3]])

            def load_out(c):
                wdma(f"wout{c % 12}", wout[c % 12], f_wo[tbase + c], x0 if c < 3 else [], [woutr[c % 12]])

            def load(c):
                load_in(c)
                load_out(c)

            for c in range(3):
                load_in(c)
            for c in range(3):
                load_out(c)
            pair = 0
            ybank = 0
            for grp in GROUPS:
                for ci, c in enumerate(grp):
                    w = win[c % 3]
                    wr = winr[c % 3]
                    for t in range(4):
                        bg, bu = (0, 1) if pair % 2 == 0 else (2, 3)
                        pair += 1
                        for k in range(8):
                            mm(banks[bg][:], w[:, k, 0:128], hT[:, k, tsl(t)], k == 0, k == 7,
                               [wr, HR[k][t]], [BR[bg]], k == 7)
                        for k in range(8):
                            mm(banks[bu][:], w[:, k, 128:256], hT[:, k, tsl(t)], k == 0, k == 7,
                               [wr, HR[k][t]], [BR[bu]], k == 7)
                        si = 2 + (pair % 2)
                        act(scr[:, si, :], banks[bg][:], AF.Silu, [BR[bg]], [SCR[si]])
                        tt("dve", actv[:, ci, tsl(t)], scr[:, si, :], banks[bu][:], ALU.mult,
                           [SCR[si], BR[bu]], [ACTR[ci][t]])
                    if c + 3 < NFC:
                        load(c + 3)
                for t in range(4):
                    for n in range(8):
                        yb = 4 + (ybank % 2)
                        ybank += 1
                        for ci, c in enumerate(grp):
                            mm(banks[yb][:], wout[c % 12][:, n * 128:(n + 1) * 128], actv[:, ci, tsl(t)],
                               ci == 0, ci == len(grp) - 1, [woutr[c % 12], ACTR[ci][t]], [BR[yb]],
                               ci == len(grp) - 1)
                        stt("dve", xT[:, n, tsl(t)], banks[yb][:], 0.5, xT[:, n, tsl(t)], ALU.mult, ALU.add,
                            [BR[yb], XR[n][t]], [XR[n][t]])
                        if next_g is not None and grp is GROUPS[-1] and t > 0:
                            norm_write(next_g, t - 1, n)
                    if next_g is not None and grp is GROUPS[-1]:
                        norm_stats(next_g, t)
                        if t == 3:
                            for k in range(8):
                                norm_write(next_g, 3, k)

        class WRing:
            def __init__(self, name, off, n, cols):
                self.name = name
                self.n = n
                self.ap, self.reg = [], []
                for i in range(n):
                    a, r = AR.region(f"{name}{i}", off + cols * i, cols)
                    self.ap.append(a)
                    self.reg.append(r)
                self.i = 0

            def load(self, src):
                i = self.i % self.n
                self.i += 1
                wdma(f"{self.name}{i}", self.ap[i], src, [], [self.reg[i]])
                return self.ap[i], self.reg[i]

            def set_queue(self, srcs):
                self.q = list(srcs)
                self.issued = 0
                self.taken = 0
                self.slots = []

            def prime(self):
                while self.issued < len(self.q) and self.issued < self.taken + self.n:
                    self.slots.append(self.load(self.q[self.issued]))
                    self.issued += 1

            def nxt(self, prime=True):
                if prime:
                    self.prime()
                r = self.slots[self.taken]
                self.taken += 1
                return r

        def qk_project(w_ap, w_reg, gcol, dst, dst_reg, dil, sq_ap, sqr, rawbanks=(0, 1, 2, 3), run=True):
            wv = w_ap.rearrange("p (k c) -> p k c", k=8)
            units = []
            for t in range(4):
                def head(t=t):
                    rb = rawbanks[t % len(rawbanks)]
                    for k in range(8):
                        mm(banks[rb][:], wv[:, k, :], hT[:, k, tsl(t)], k == 0, k == 7,
                           [w_reg, HR[k][t]], [BR[rb]], k == 7)
                    i = t % 2
                    act(sq_ap[i], banks[rb][:], AF.Square, [BR[rb]], [sqr[i]])

                def tail(t=t):
                    rb = rawbanks[t % len(rawbanks)]
                    i = t % 2
                    sbk = 6 + (t % 2)
                    mm(banks[sbk][:], bd_bf, sq_ap[i], True, True, [CB, sqr[i]], [BR[sbk]], True)
                    si = 6 + (t % 2)
                    rsqrt_act(scr[:, si, :], banks[sbk][:], 1.0 / HD, [BR[sbk]], SCR[si])
                    if dil == 1:
                        o = dst[:, tsl(t)]
                        i0 = banks[rb][:]
                        i1 = scr[:, si, :]
                    else:
                        n_per = 512 // dil
                        o = dst.rearrange("p (c i) -> p c i", c=dil)[:, :, n_per * t:n_per * (t + 1)]
                        i0 = banks[rb][:].rearrange("p (i c) -> p c i", c=dil)
                        i1 = scr[:, si, :].rearrange("p (i c) -> p c i", c=dil)
                    stt("dve", o, i0, cs(gcol), i1, ALU.mult, ALU.mult, [BR[rb], SCR[si], CST], [dst_reg])
                units.append((head, tail))
            if not run:
                return units
            pipeline(units)

        def dilated(l, odil, ODR, mw):
            off = BASE + 4096
            qk_ap, qk_reg = {}, {}
            for g in range(3):
                for which in range(2):
                    a, r = AR.region(f"dqk{g}{which}", off, 2048)
                    qk_ap[(g, which)] = a
                    qk_reg[(g, which)] = r
                    off += 2048
            v_ap, v_reg = {}, {}
            for g in range(3):
                a, r = AR.region(f"dv{g}", off, 4096)
                v_ap[g] = a.rearrange("p (b h e) -> p b h e", b=16, h=2)
                v_reg[g] = r
                off += 4096
            ring = mw
            e_ap, e_reg = [], []
            for i in range(6):
                a, r = AR.region(f"dE{i}", off, 512)
                e_ap.append(a)
                e_reg.append(r)
                off += 512
            sq_ap, sqr = [], []
            for i in range(2):
                a, r = AR.region(f"dsq{i}", off, 512)
                sq_ap.append(a)
                sqr.append(r)
                off += 512
            assert off <= NAR - 4096
            ecnt = [0]

            for jp in range(2):
                for g in range(3):
                    if jp != 0:
                        break
                    P.op("pool", (lambda g: lambda e: e.memset(v_ap[g][:, :, 0, 64:128], 1.0))(g), [], [v_reg[g]])
                    P.op("pool", (lambda g: lambda e: e.memset(v_ap[g][:, :, 1, 0:64], 1.0))(g), [], [v_reg[g]])
                for g in range(3):
                    dil = DILS[g]
                    units = []
                    for which in range(2):
                        w_ap, w_reg = ring.nxt(prime=(which == 0))
                        gcol = C_QKDIL + ((l * 2 + which) * 3 + g) * 2 + jp
                        units += qk_project(w_ap, w_reg, gcol, qk_ap[(g, which)], qk_reg[(g, which)], dil, sq_ap, sqr,
                                            run=False)
                    pipeline(units)
                    w_ap, w_reg = ring.nxt()
                    wv = w_ap.rearrange("p (k c) -> p k c", k=8)
                    for b4 in range(4):
                        vb = 4 + (b4 % 2)
                        for bb in range(4):
                            blk = b4 * 4 + bb
                            nblk = 16 // dil
                            c, n = blk // nblk, blk % nblk
                            start_tok = 128 * n * dil + c
                            for k in range(8):
                                lhsT = hT[:, k, start_tok:start_tok + 127 * dil + 1:dil] if dil > 1 else \
                                    hT[:, k, start_tok:start_tok + 128]
                                mm(banks[vb][:, bb * 128:(bb + 1) * 128], lhsT, wv[:, k, :], k == 0, k == 7,
                                   [w_reg] + [HR[k][tt_] for tt_ in range(4)], [BR[vb]],
                                   (k == 7 and bb == 3))
                        src = banks[vb][:].rearrange("p (b h e) -> p b h e", b=4, h=2)
                        P.op("act", (lambda g, b4, src: lambda e: e.activation(
                            out=v_ap[g][:, 4 * b4:4 * b4 + 4, 0, 0:64], in_=src[:, :, 0, :], func=AF.Copy))(g, b4, src),
                            [BR[vb]], [v_reg[g]])
                        P.op("act", (lambda g, b4, src: lambda e: e.activation(
                            out=v_ap[g][:, 4 * b4:4 * b4 + 4, 1, 64:128], in_=src[:, :, 1, :], func=AF.Copy))(g, b4, src),
                            [BR[vb]], [v_reg[g]])
                units = []
                for hl in range(2):
                    j = 2 * jp + hl
                    ps = slice(64 * hl, 64 * hl + 64)
                    num = slice(0, 64) if hl == 0 else slice(64, 128)
                    den = slice(64, 128) if hl == 0 else slice(0, 64)
                    for n in range(4):
                        ob = 4 + (n % 2)
                        jobs = []
                        for which, mid in ((0, 0), (1, 1)):
                            items = []
                            for qb in range(4):
                                b = 4 * n + qb
                                kb = b - 1 if which == 0 else b
                                if kb < 0:
                                    continue
                                items.append((qk_ap[(0, 1)][ps, 128 * kb:128 * kb + 128],
                                              qk_ap[(0, 0)][ps, 128 * b:128 * b + 128],
                                              kb, banks[ob][:, 128 * qb:128 * qb + 128], 128, 128 * qb))
                            jobs.append((0, mid, items))
                        for which, mid in ((0, 0), (1, 1)):
                            items = []
                            for c in range(4):
                                kn = n - 1 if which == 0 else n
                                if kn < 0:
                                    continue
                                items.append((qk_ap[(1, 1)][ps, 512 * c + 128 * kn:512 * c + 128 * kn + 128],
                                              qk_ap[(1, 0)][ps, 512 * c + 128 * n:512 * c + 128 * n + 128],
                                              4 * c + kn, banks[ob][:, c:512:4], 128, 128 * c))
                            jobs.append((1, mid, items))
                        items = []
                        for c in range(16):
                            items.append((qk_ap[(2, 1)][ps, 128 * c:128 * c + 128],
                                          qk_ap[(2, 0)][ps, 128 * c + 32 * n:128 * c + 32 * n + 32],
                                          c, banks[ob][:, c:512:16], 32, 32 * c))
                        jobs.append((2, 2, items))
                        jobs = [jb for jb in jobs if jb[2]]
                        for ji, (g, mid, items) in enumerate(jobs):
                            sbk = ecnt[0] % 4
                            ei = ecnt[0] % 6
                            ecnt[0] += 1
                            first = ji == 0
                            last = ji == len(jobs) - 1

                            def head(g=g, items=items, sbk=sbk, first=first, ob=ob):
                                if first:
                                    P.op("dve", lambda e: e.memset(banks[ob][:], 0.0), [], [BR[ob]])
                                for ii, (k_ap, q_ap, vblk, dest, ncols, col0) in enumerate(items):
                                    mm(banks[sbk][:, col0:col0 + ncols], k_ap, q_ap, True, True,
                                       [qk_reg[(g, 1)], qk_reg[(g, 0)]], [BR[sbk]], ii == len(items) - 1, skip=True)

                            def tail(g=g, mid=mid, items=items, sbk=sbk, ei=ei, first=first, last=last,
                                     ob=ob, n=n, j=j, hl=hl, num=num, den=den):
                                lo = items[0][5]
                                hi = items[-1][5] + items[-1][4]
                                act(e_ap[ei][:, lo:hi], banks[sbk][:, lo:hi], AF.Exp, [BR[sbk]], [e_reg[ei]],
                                    scale=0.125)
                                if mid == 2:
                                    mk = maskv[:, 2 * j + 1:2 * j + 2, 32 * n:32 * n + 32].broadcast_to([128, 16, 32])
                                    ev = e_ap[ei][:, :].rearrange("p (c q) -> p c q", c=16)
                                else:
                                    nb = (hi - lo) // 128
                                    mk = maskv[:, 2 * j + mid:2 * j + mid + 1, :].broadcast_to([128, nb, 128])
                                    ev = e_ap[ei][:, lo:hi].rearrange("p (c q) -> p c q", c=nb)
                                tt("dve", ev, ev, mk, ALU.mult, [e_reg[ei], CB], [e_reg[ei]])
                                for ii, (k_ap, q_ap, vblk, dest, ncols, col0) in enumerate(items):
                                    mm(dest, v_ap[g][:, vblk, hl, :], e_ap[ei][:, col0:col0 + ncols], False, False,
                                       [v_reg[g], e_reg[ei]], [BR[ob]], ii == len(items) - 1, skip=True)
                                if deferred:
                                    deferred.pop(0)()
                                if last:
                                    def fin_piece(ob=ob, n=n, num=num, den=den):
                                        si = 4 + (n % 2)
                                        recip_act(scr[num, si, :], banks[ob][den, :], [BR[ob]], SCR[si])
                                        tt("dve", odil[num, jp, tsl(n)], banks[ob][num, :], scr[num, si, :], ALU.mult,
                                           [BR[ob], SCR[si]], [ODR[jp]])
                                    deferred.append(fin_piece)
                            units.append((head, tail))
                deferred = []
                pipeline(units, depth=2)
                while deferred:
                    deferred.pop(0)()

        def diffattn(l, odiff, OFR, mw, before_last_attn=None):
            off = BASE + 4096 + 8192
            q_ap, q_reg, k_ap, k_reg, v_ap, v_reg = [], [], [], [], [], []
            for i in range(2):
                a, r = AR.region(f"fq{i}", off, 2048)
                q_ap.append(a)
                q_reg.append(r)
                off += 2048
                a, r = AR.region(f"fk{i}", off, 2048)
                k_ap.append(a)
                k_reg.append(r)
                off += 2048
                a, r = AR.region(f"fv{i}", off, 2048)
                v_ap.append(a.rearrange("p (b e) -> p b e", b=16))
                v_reg.append(r)
                off += 2048
            ring = mw
            ep_ap, ep_reg = [], []
            for i in range(3):
                a, r = AR.region(f"fEp{i}", off, 1024)
                ep_ap.append(a.rearrange("p (m q) -> p m q", m=2))
                ep_reg.append(r)
                off += 1024
            sq_ap, sqr = [], []
            for i in range(2):
                a, r = AR.region(f"fsq{i}", off, 512)
                sq_ap.append(a)
                sqr.append(r)
                off += 512
            assert off <= NAR
            P.op("dve", lambda e: e.tensor_tensor(out=lamprod[:, 0:128],
                                                  in0=cst[:, C_LQ + l * 128:C_LQ + l * 128 + 128],
                                                  in1=cst[:, C_LK + l * 128:C_LK + l * 128 + 128], op=ALU.mult),
                 [CST], [LAM])
            P.op("dve", lambda e: e.tensor_reduce(out=lamtmp[:, 0:2],
                                                  in_=lamprod[:, 0:128].rearrange("p (a b) -> p a b", a=2),
                                                  axis=mybir.AxisListType.X, op=ALU.add), [LAM], [LAM])
            act(lamtmp[:, 2:4], lamtmp[:, 0:2], AF.Exp, [LAM], [LAM])
            tt("dve", lamtmp[:, 4:5], lamtmp[:, 3:4], lamtmp[:, 2:3], ALU.subtract, [LAM], [LAM])
            tt("dve", cst[:, C_NLAM + l:C_NLAM + l + 1], lamtmp[:, 4:5], cs(C_LAMI + l), ALU.subtract,
               [LAM, CST], [CST])
            P.op("dve", lambda e: e.tensor_scalar(out=cst[:, C_SUBS + 4 * l:C_SUBS + 4 * l + 4],
                                                  in0=cst[:, C_SUB + 4 * l:C_SUB + 4 * l + 4],
                                                  scalar1=cs(C_OML + l), scalar2=None, op0=ALU.mult),
                 [CST], [CST])

            cnt = [0]
            ecnt = [0]
            deferred = []
            pre = []
            for h in range(4):
                i = h % 2
                w_ap, w_reg = ring.nxt()
                pu = qk_project(w_ap, w_reg, C_QKDIFF + (l * 2 + 0) * 4 + h, q_ap[i], q_reg[i], 1, sq_ap, sqr, run=False)
                w_ap, w_reg = ring.nxt(prime=False)
                pu += qk_project(w_ap, w_reg, C_QKDIFF + (l * 2 + 1) * 4 + h, k_ap[i], k_reg[i], 1, sq_ap, sqr, run=False)
                pipeline(pu)
                while deferred:
                    deferred.pop(0)(7)
                w_ap, w_reg = ring.nxt()
                wv = w_ap.rearrange("p (k c) -> p k c", k=8)
                for b4 in range(4):
                    vb = 2 + (b4 % 2)
                    for bb in range(4):
                        blk = b4 * 4 + bb
                        for k in range(8):
                            mm(banks[vb][:, bb * 128:(bb + 1) * 128], hT[:, k, 128 * blk:128 * blk + 128],
                               wv[:, k, :], k == 0, k == 7, [w_reg, HR[k][blk // 4]], [BR[vb]],
                               (k == 7 and bb == 3))
                    src = banks[vb][:].rearrange("p (b e) -> p b e", b=4)
                    P.op("act", (lambda i, b4, src: lambda e: e.activation(
                        out=v_ap[i][:, 4 * b4:4 * b4 + 4, :], in_=src, func=AF.Copy))(i, b4, src),
                        [BR[vb]], [v_reg[i]])
                if h == 3 and before_last_attn is not None:
                    before_last_attn()
                units = []
                SBP = (0, 6)
                tri2 = tri_bf.unsqueeze(1).broadcast_to([128, 2, 128])
                for t in range(4):
                    nkb = 4 * t + 4
                    for kb in range(nkb):
                        dk = kb - 4 * t
                        qlo = 0 if dk < 0 else 128 * dk
                        fin = kb == nkb - 1
                        slot = {}

                        def head(kb=kb, qlo=qlo, slot=slot, t=t):
                            sb0 = slot["sb0"] = SBP[cnt[0] % 2]
                            cnt[0] += 1
                            for m in range(2):
                                ps = slice(64 * m, 64 * m + 64)
                                mm(banks[sb0 + m][:, qlo:512], k_ap[i][ps, 128 * kb:128 * kb + 128],
                                   q_ap[i][ps, 512 * t + qlo:512 * t + 512], True, True,
                                   [k_reg[i], q_reg[i]], [BR[sb0 + m]], True)

                        def tail(kb=kb, dk=dk, qlo=qlo, slot=slot, t=t, nkb=nkb, fin=fin):
                            sb0 = slot["sb0"]
                            ei = ecnt[0] % 3
                            ecnt[0] += 1
                            ep = ep_ap[ei]
                            act(ep[:, :, qlo:512], ps_all[:, sb0:sb0 + 2, qlo:512], AF.Exp,
                                [BR[sb0], BR[sb0 + 1], CST], [ep_reg[ei]],
                                bias=cs(C_ALIBI + h * 19 + (dk + 15)), scale=0.125)
                            had_pre = bool(pre)
                            while pre:
                                pre.pop(0)()
                            if dk >= 0:
                                tt("dve", ep[:, :, qlo:qlo + 128], ep[:, :, qlo:qlo + 128], tri2, ALU.mult,
                                   [ep_reg[ei], CB], [ep_reg[ei]])
                            order = ((2, 0), (2, 1), (4, 0), (4, 1)) if had_pre else ((4, 0), (2, 0), (4, 1), (2, 1))
                            for (bb, m) in order:
                                if bb == 4:
                                    mm(banks[4 + m][:, qlo:512], ones_bf, ep[:, m, qlo:512], kb == 0, kb == nkb - 1,
                                       [CB, ep_reg[ei]], [BR[4 + m]], True, skip=True)
                                else:
                                    mm(banks[2 + m][:, qlo:512], v_ap[i][:, kb, :], ep[:, m, qlo:512], kb == 0,
                                       kb == nkb - 1, [v_reg[i], ep_reg[ei]], [BR[2 + m]], True, skip=True)
                            if deferred:
                                deferred.pop(0)(sb0)
                            if not fin:
                                return
                            for m in range(2):
                                P.op("dve", (lambda m: lambda e: e.tensor_copy(out=scr[:, m, :], in_=banks[2 + m][:]))(m),
                                     [BR[2 + m]], [SCR[m]])
                                pre.append((lambda m: lambda: act(scr[:, 2 + m, :], banks[4 + m][:], AF.Ln, [BR[4 + m]],
                                                                  [SCR[2 + m]]))(m))

                            def piece_b1(fb):
                                act(scr[:, 2, :], scr[:, 2, :], AF.Exp, [SCR[2]], [SCR[2]], scale=-1.0)
                                tt("dve", scr[:, 0, :], scr[:, 0, :], scr[:, 2, :], ALU.mult, [SCR[0], SCR[2]], [SCR[0]])

                            def piece_b2(fb):
                                act(scr[:, 3, :], scr[:, 3, :], AF.Exp, [SCR[3]], [SCR[3]], scale=-1.0)
                                tt("dve", scr[:, 1, :], scr[:, 1, :], scr[:, 3, :], ALU.mult, [SCR[1], SCR[3]], [SCR[1]])
                                stt("dve", scr[:, 0, :], scr[:, 1, :], cs(C_NLAM + l), scr[:, 0, :], ALU.mult, ALU.add,
                                    [SCR[0], SCR[1], CST], [SCR[0]])

                            def piece_c1(fb, t=t):
                                si = t % 2
                                tt("dve", sq_ap[si], scr[:, 0, :], scr[:, 0, :], ALU.mult, [SCR[0]], [sqr[si]])
                                mm(banks[fb][:], ones_bf, sq_ap[si], True, True, [CB, sqr[si]], [BR[fb]], True)
                                act(scr[:, 2, :], banks[fb][:], AF.Ln, [BR[fb], CST], [SCR[2]], bias=cs(C_EPS), scale=1.0 / 128)

                            def piece_c2(fb, t=t, h=h):
                                act(scr[:, 2, :], scr[:, 2, :], AF.Exp, [SCR[2]], [SCR[2]], scale=-0.5)
                                stt("dve", odiff[:, h, tsl(t)], scr[:, 0, :], cs(C_SUBS + 4 * l + h), scr[:, 2, :],
                                    ALU.mult, ALU.mult, [SCR[0], SCR[2], CST], [OFR[h]])
                            deferred.extend([piece_b1, piece_b2, piece_c1, piece_c2])
                        units.append((head, tail))
                pipeline(units, depth=1)
                while pre:
                    pre.pop(0)()
                if h == 3:
                    while deferred:
                        deferred.pop(0)(0)

        def final(l, odil, ODR, odiff, OFR, ring, wo_ap, wo_regs, next_g=None):
            off = BASE + 4096 + 8192
            y_ap, y_rs = AR.multi("yT", off, 16384, 32)
            yT = y_ap.rearrange("p (m s) -> p m s", m=8)
            YR = [[y_rs[m * 4 + t] for t in range(4)] for m in range(8)]
            wov = wo_ap.rearrange("p (m c) -> p m c", m=8)
            for m in range(8):
                w_ap, w_reg = ring.nxt()
                ga = w_ap[:, 0:1024].rearrange("p (k c) -> p k c", k=8)
                gb = w_ap[:, 1024:2048].rearrange("p (k c) -> p k c", k=8)
                wa = w_ap[:, 2048:2304].rearrange("p (j c) -> p j c", j=2)
                wb = w_ap[:, 2560:3072].rearrange("p (h c) -> p h c", h=4)
                for t in range(4):
                    b0 = 0 if t % 2 == 0 else 4
                    for k in range(8):
                        mm(banks[b0][:], ga[:, k, :], hT[:, k, tsl(t)], k == 0, k == 7, [w_reg, HR[k][t]],
                           [BR[b0]], k == 7)
                    for k in range(8):
                        mm(banks[b0 + 1][:], gb[:, k, :], hT[:, k, tsl(t)], k == 0, k == 7, [w_reg, HR[k][t]],
                           [BR[b0 + 1]], k == 7)
                    for jp in range(2):
                        mm(banks[b0 + 2][:], wa[:, jp, :], odil[:, jp, tsl(t)], jp == 0, jp == 1,
                           [w_reg, ODR[jp]], [BR[b0 + 2]], jp == 1)
                    for h in range(4):
                        mm(banks[b0 + 3][:], wb[:, h, :], odiff[:, h, tsl(t)], h == 0, h == 3,
                           [w_reg, OFR[h]], [BR[b0 + 3]], h == 3)
                    s0 = 0 if t % 2 == 0 else 2
                    act(scr[:, s0, :], banks[b0][:], AF.Sigmoid, [BR[b0]], [SCR[s0]])
                    act(scr[:, s0 + 1, :], banks[b0 + 1][:], AF.Sigmoid, [BR[b0 + 1]], [SCR[s0 + 1]])
                    tt("dve", scr[:, s0, :], scr[:, s0, :], banks[b0 + 2][:], ALU.mult, [SCR[s0], BR[b0 + 2]], [SCR[s0]])
                    tt("dve", scr[:, s0 + 1, :], scr[:, s0 + 1, :], banks[b0 + 3][:], ALU.mult,
                       [SCR[s0 + 1], BR[b0 + 3]], [SCR[s0 + 1]])
                    tt("dve", yT[:, m, tsl(t)], scr[:, s0, :], scr[:, s0 + 1, :], ALU.add,
                       [SCR[s0], SCR[s0 + 1]], [YR[m][t]])
            ob = 0
            for t in range(4):
                for n in range(8):
                    b = ob % 8
                    ob += 1
                    for m in range(8):
                        mm(banks[b][:], wov[:, m, n * 128:(n + 1) * 128], yT[:, m, tsl(t)], m == 0, m == 7,
                           [wo_regs[m // 4], YR[m][t]], [BR[b]], m == 7)
                    tt("dve", xT[:, n, tsl(t)], banks[b][:], xT[:, n, tsl(t)], ALU.add, [BR[b], XR[n][t]], [XR[n][t]])
                    if next_g is not None and t > 0:
                        norm_write(next_g, t - 1, n)
                if next_g is not None:
                    norm_stats(next_g, t)
                    if t == 3:
                        for k in range(8):
                            norm_write(next_g, 3, k)

        def mixer(l, do_norm=True, next_g=None):
            if do_norm:
                rmsnorm(C_NORM + (l * 3 + 1) * 8)
            od_ap, ODR = AR.multi("odil", BASE, 4096, 2)
            odil = od_ap.rearrange("p (j s) -> p j s", j=2)
            of_ap, OFR = AR.multi("odiff", BASE + 4096, 8192, 4)
            odiff = of_ap.rearrange("p (h s) -> p h s", h=4)
            mw = WRing("mw", NAR - 4096, 4, 1024)
            tiles = []
            if "dil" in mix_parts:
                for jp in range(2):
                    for g in range(3):
                        tiles += [g * 2 + jp, 6 + g * 2 + jp, 12 + g * 2 + jp]
            if "diff" in mix_parts:
                for h in range(4):
                    tiles += [18 + h, 22 + h, 26 + h]
            mw.set_queue([m_wi[l * 30 + t_] for t_ in tiles])
            mw.prime()
            if "dil" in mix_parts:
                dilated(l, odil, ODR, mw)
            fin = {}

            def prefetch_final():
                fin["ring"] = WRing("nw", BASE + 4096 + 8192 + 16384, 2, 3072)
                fin["ring"].set_queue([m_fin[l * 8 + m] for m in range(8)])
                fin["ring"].prime()
                off = BASE + 4096 + 8192 + 16384 + 6144
                a0, r0 = AR.region("wo0", off, 4096)
                a1, r1 = AR.region("wo1", off + 4096, 4096)
                fin["wo_ap"] = ar_t[:, off:off + 8192]
                fin["wo_regs"] = [r0, r1]
                wdma("wo_a", a0, m_wo[l][:, 0:4096], [], [r0])
                wdma("wo_b", a1, m_wo[l][:, 4096:8192], [], [r1])

            if "diff" in mix_parts:
                diffattn(l, odiff, OFR, mw, prefetch_final if "fin" in mix_parts else None)
            if "fin" in mix_parts:
                if not fin:
                    prefetch_final()
                final(l, odil, ODR, odiff, OFR, fin["ring"], fin["wo_ap"], fin["wo_regs"], next_g)

        plan = []
        for l in range(n_layers):
            for st_name in ("ffn1", "mix", "ffn2"):
                if st_name in stages:
                    plan.append((st_name, l))
        fuse_norm = "fin" in mix_parts

        def gcol_of(st_name, l):
            return C_NORM + (l * 3 + {"ffn1": 0, "mix": 1, "ffn2": 2}[st_name]) * 8

        for i, (st_name, l) in enumerate(plan):
            do_norm = (i == 0) or not fuse_norm
            after = None
            if fuse_norm and i + 1 < len(plan):
                after = gcol_of(*plan[i + 1])
            if st_name == "ffn1":
                ffn(l, 0, do_norm, after)
            elif st_name == "mix":
                mixer(l, do_norm, after)
            else:
                ffn(l, 1, do_norm, after)

        out_dv = out_d.rearrange("n p s -> p n s")
        for t in range(4):
            for hf in range(2):
                P.dma("sp", "out",
                      (lambda t, hf: lambda e: e.dma_start(out=out_dv[:, 4 * hf:4 * hf + 4, 512 * t:512 * t + 512],
                                                           in_=xT[:, 4 * hf:4 * hf + 4, 512 * t:512 * t + 512]))(t, hf),
                      reads=[XR[n][t] for n in range(4 * hf, 4 * hf + 4)], writes=[OUT])
        P.wait_all("sp", [OUT])
        P.emit(nc, st)
    return nc


def _prep_weights(inp):
    f32 = np.float32
    f_wi = np.empty((L * 2 * NFC, 128, 2048), f32)
    f_wo = np.empty((L * 2 * NFC, 128, 1024), f32)
    for l in range(L):
        for f, (wi, wo) in enumerate(((inp["ffn1_w_in"], inp["ffn1_w_out"]), (inp["ffn2_w_in"], inp["ffn2_w_out"]))):
            w = np.asarray(wi[l], f32)
            g = w[:, :FF].reshape(8, 128, NFC, 128)
            u = w[:, FF:].reshape(8, 128, NFC, 128)
            t = np.stack([g, u], axis=3)
            t = t.transpose(2, 1, 0, 3, 4).reshape(NFC, 128, 2048)
            base = (l * 2 + f) * NFC
            f_wi[base:base + NFC] = t
            f_wo[base:base + NFC] = np.asarray(wo[l], f32).reshape(NFC, 128, 1024)
    m_wi = np.empty((L * 30, 128, 1024), f32)
    m_fin = np.zeros((L * 8, 128, 3072), f32)
    m_wo = np.empty((L, 128, 8192), f32)
    for l in range(L):
        w = np.asarray(inp["w_in"][l], f32)
        tiles = w.reshape(8, 128, 46, 128).transpose(2, 1, 0, 3).reshape(46, 128, 1024)
        m_wi[l * 30:(l + 1) * 30] = tiles[:30]
        wa = np.asarray(inp["w_branch_dil"][l], f32)
        wb = np.asarray(inp["w_branch_diff"][l], f32)
        for m in range(8):
            m_fin[l * 8 + m, :, 0:1024] = tiles[30 + m]
            m_fin[l * 8 + m, :, 1024:2048] = tiles[38 + m]
            a = wa[:, m * 128:(m + 1) * 128].reshape(2, 128, 128).transpose(1, 0, 2).reshape(128, 256)
            m_fin[l * 8 + m, :, 2048:2304] = a
            b = wb[:, m * 128:(m + 1) * 128].reshape(4, 128, 128).transpose(1, 0, 2).reshape(128, 512)
            m_fin[l * 8 + m, :, 2560:3072] = b
        m_wo[l] = np.asarray(inp["w_out"][l], f32).reshape(8, 128, 1024).transpose(1, 0, 2).reshape(128, 8192)
    return f_wi, f_wo, m_wi, m_fin, m_wo


def _prep_consts(inp):
    f32 = np.float32
    cst = np.zeros((128, NCST), f32)
    norms = (inp["ffn1_norm"], inp["mix_norm"], inp["ffn2_norm"])
    for l in range(L):
        for w in range(3):
            cst[:, C_NORM + (l * 3 + w) * 8:C_NORM + (l * 3 + w) * 8 + 8] = \
                np.asarray(norms[w][l], f32).reshape(8, 128).T
        gd = np.asarray(inp["qk_gain_dil"][l], f32)
        for qk in range(2):
            for g in range(3):
                for jp in range(2):
                    cst[:, C_QKDIL + ((l * 2 + qk) * 3 + g) * 2 + jp] = gd[qk, g, 2 * jp:2 * jp + 2].reshape(128)
        gf = np.asarray(inp["qk_gain_diff"][l], f32)
        for qk in range(2):
            for h in range(4):
                cst[:, C_QKDIFF + (l * 2 + qk) * 4 + h] = gf[qk, h].reshape(128)
        cst[:, C_SUB + 4 * l:C_SUB + 4 * l + 4] = np.asarray(inp["diff_subnorm"][l], f32).T
        lam_init = 0.8 - 0.6 * math.exp(-0.3 * l)
        cst[:, C_LAMI + l] = lam_init
        cst[:, C_OML + l] = 1.0 - lam_init
        cst[:, C_LQ + l * 128:C_LQ + (l + 1) * 128] = np.asarray(inp["lambda_q"][l], f32).reshape(1, 128)
        cst[:, C_LK + l * 128:C_LK + (l + 1) * 128] = np.asarray(inp["lambda_k"][l], f32).reshape(1, 128)
    cst[:, C_EPS] = EPS
    sl = _slopes()
    p = np.arange(128, dtype=np.float64)
    for h in range(4):
        for d in range(19):
            cst[:, C_ALIBI + h * 19 + d] = (sl[12 + h] * (p + 128.0 * (d - 15))).astype(f32)
    cb = np.zeros((128, NCB), f32)
    cb[:, 0:128] = 1.0
    cb[0:64, 128:192] = 1.0
    cb[64:128, 192:256] = 1.0
    k = np.arange(128)[:, None].astype(np.float64)
    q = np.arange(128)[None, :].astype(np.float64)
    for j in range(4):
        s = 2.0 ** (-(j + 1) / 2.0)
        prev = np.where(q <= k, np.exp(-s * (128.0 + q - k)), 0.0)
        cur = np.where(q >= k, np.exp(-s * (q - k)), 0.0)
        cb[:, 256 + (2 * j) * 128:256 + (2 * j + 1) * 128] = prev
        cb[:, 256 + (2 * j + 1) * 128:256 + (2 * j + 2) * 128] = cur
    cb[:, 1280:1408] = (q >= k)
    return cst, cb


_NC_CACHE = {}


def kernel(**inputs):
    x = np.asarray(inputs["x"], np.float32)
    f_wi, f_wo, m_wi, m_fin, m_wo = _prep_weights(inputs)
    cst, cb = _prep_consts(inputs)
    if "nc" not in _NC_CACHE:
        _NC_CACHE["nc"] = build_nc()
    nc = _NC_CACHE["nc"]
    in_maps = []
    for b in range(8):
        xT = np.ascontiguousarray(x[b].T).reshape(8, 128, S)
        in_maps.append({"xT": xT, "f_wi": f_wi, "f_wo": f_wo, "m_wi": m_wi, "m_fin": m_fin,
                        "m_wo": m_wo, "cst": cst, "cb": cb})
    res = run_bass_kernel_spmd(nc, in_maps, core_ids=list(range(8)))
    out = np.empty((8, S, D), np.float32)
    for b in range(8):
        out[b] = np.asarray(res.results[b]["outT"], np.float32).reshape(D, S).T
    return out
```

```python
import math
from contextlib import ExitStack

import numpy as np
import concourse.bass as bass
import concourse.mybir as mybir
from concourse.bass_utils import run_bass_kernel_spmd

F32 = mybir.dt.float32
BF16 = mybir.dt.bfloat16
AF = mybir.ActivationFunctionType
ALU = mybir.AluOpType

D = 1024
S = 2048
L = 2
FF = 2816
NFC = 22
HD = 64
EPS = 1e-6
GROUPS = [list(range(0, 6)), list(range(6, 12)), list(range(12, 17)), list(range(17, 22))]
DILS = (1, 4, 16)


class Reg:
    __slots__ = ("name", "w", "r", "excl")

    def __init__(self, name, excl=False):
        self.name = name
        self.w = None
        self.r = []
        self.excl = excl


class Eng:
    def __init__(self, key):
        self.key = key
        self.count = 0
        self.pending = False
        self.seen = {}
        self.ops = []


class Prog:
    ENGS = ("pe", "act", "dve", "pool", "sp")

    def __init__(self):
        self.e = {k: Eng(k) for k in self.ENGS}
        self.dma_sems = {}

    def _collect(self, eng, reads, writes):
        need = {}

        def add(ev, same_ok):
            if ev is None:
                return
            k, v = ev
            if k == eng.key and not same_ok:
                return
            if need.get(k, 0) < v:
                need[k] = v

        for r in reads:
            add(r.w, True)
            if r.excl:
                for ev in r.r:
                    add(ev, False)
        same = eng.key != "pe"
        for w in writes:
            add(w.w, same)
            for ev in w.r:
                add(ev, same)
        waits = []
        for k, v in need.items():
            if eng.seen.get(k, 0) >= v:
                continue
            eng.seen[k] = v
            waits.append((k, v))
        return waits

    def op(self, engkey, fn, reads=(), writes=(), inc=True):
        eng = self.e[engkey]
        waits = self._collect(eng, reads, writes)
        ev = (engkey, eng.count + 1)
        if inc:
            eng.count += 1
            eng.pending = False
        else:
            eng.pending = True
        for r in reads:
            r.r.append(ev)
            if len(r.r) > 64:
                r.r = _compress(r.r)
        for w in writes:
            w.w = ev
            w.r = []
        eng.ops.append((waits, fn, inc, None))

    def dma(self, engkey, semname, fn, reads=(), writes=()):
        eng = self.e[engkey]
        waits = self._collect(eng, reads, writes)
        self.dma_sems[semname] = self.dma_sems.get(semname, 0) + 16
        ev = ("dma:" + semname, self.dma_sems[semname])
        for r in reads:
            r.r.append(ev)
        for w in writes:
            w.w = ev
            w.r = []
        eng.ops.append((waits, fn, False, semname))

    def wait_all(self, engkey, regs):
        eng = self.e[engkey]
        waits = self._collect(eng, regs, regs)
        eng.ops.append((waits, None, False, None))

    def emit(self, nc, stack):
        sems = {}
        for k in self.ENGS:
            sems[k] = stack.enter_context(nc.semaphore("s_" + k))
        for name in self.dma_sems:
            sems["dma:" + name] = stack.enter_context(nc.semaphore("d_" + name))
        for k in self.ENGS:
            assert not self.e[k].pending, k
        block = stack.enter_context(nc.Block())
        hooks = {"pe": block.tensor, "act": block.scalar, "dve": block.vector,
                 "pool": block.gpsimd, "sp": block.sync}

        def make(k):
            ops = self.e[k].ops

            def body(h):
                for waits, fn, inc, dsem in ops:
                    if fn is None:
                        for (sk, v) in waits:
                            h.wait_ge(sems[sk], v)
                        continue
                    for (sk, v) in waits[:-1]:
                        h.wait_ge(sems[sk], v)
                    ins = fn(h)
                    if waits:
                        ins._wait_ge(sems[waits[-1][0]], waits[-1][1])
                    if inc:
                        ins.then_inc(sems[k], 1)
                    if dsem is not None:
                        ins.then_inc(sems["dma:" + dsem], 16)
            return body

        for k in self.ENGS:
            hooks[k](make(k))


def _compress(evs):
    best = {}
    for k, v in evs:
        if best.get(k, 0) < v:
            best[k] = v
    return list(best.items())


class Arena:
    def __init__(self, tensor, ncols):
        self.t = tensor
        self.n = ncols
        self.live = []

    def region(self, name, off, ncols):
        assert off + ncols <= self.n, (name, off, ncols, self.n)
        reg = Reg(name)
        keep = []
        for (a, b, r) in self.live:
            if a < off + ncols and off < b:
                if r.w is not None:
                    reg.r.append(r.w)
                reg.r.extend(r.r)
            else:
                keep.append((a, b, r))
        reg.r = _compress(reg.r)
        keep.append((off, off + ncols, reg))
        self.live = keep
        return self.t[:, off:off + ncols], reg

    def multi(self, name, off, ncols, count):
        ap, parent = self.region(name, off, ncols)
        regs = []
        for i in range(count):
            r = Reg(f"{name}.{i}")
            r.r = list(parent.r)
            regs.append(r)
            self.live.append((off, off + ncols, r))
        return ap, regs


C_NORM = 0
C_QKDIL = C_NORM + 48
C_QKDIFF = C_QKDIL + 24
C_SUB = C_QKDIFF + 16
C_EPS = C_SUB + 8
C_LAMI = C_EPS + 1
C_OML = C_LAMI + 2
C_ALIBI = C_OML + 2
C_LQ = C_ALIBI + 76
C_LK = C_LQ + 256
NCST = C_LK + 256
C_NLAM = NCST
C_SUBS = C_NLAM + 2
NCST_ALL = C_SUBS + 8

NCB = 13 * 128


def _slopes():
    return [2.0 ** (-8.0 * (i + 1) / 16.0) for i in range(16)]


def build_nc(n_layers=L, stages=("ffn1", "mix", "ffn2"), mix_parts=("dil", "diff", "fin")):
    nc = bass.Bass("TRN2", target_bir_lowering=False)
    dt_in = lambda name, shape: nc.dram_tensor(name, shape, F32, kind="ExternalInput").ap()
    xT_d = dt_in("xT", [8, 128, S])
    f_wi = dt_in("f_wi", [L * 2 * NFC, 128, 2048])
    f_wo = dt_in("f_wo", [L * 2 * NFC, 128, 1024])
    m_wi = dt_in("m_wi", [L * 30, 128, 1024])
    m_fin = dt_in("m_fin", [L * 8, 128, 3072])
    m_wo = dt_in("m_wo", [L, 128, 8192])
    cst_d = dt_in("cst", [128, NCST])
    cb_d = dt_in("cb", [128, NCB])
    out_d = nc.dram_tensor("outT", [8, 128, S], F32, kind="ExternalOutput").ap()

    P = Prog()
    NAR = 16384 + 43008
    with ExitStack() as st:
        sb = lambda n, sh, dt: st.enter_context(nc.sbuf_tensor(n, sh, dt))
        xT = sb("xT_sb", [128, 8, S], F32)
        ar_t = sb("arena", [128, NAR], BF16)
        scr = sb("scr", [128, 8, 512], F32)
        cst = sb("cst_sb", [128, NCST_ALL], F32)
        cb = sb("cb_sb", [128, NCB], BF16)
        lamtmp = sb("lamtmp", [128, 8], F32)
        lamprod = sb("lamprod", [128, 256], F32)
        ps_all = st.enter_context(nc.psum_tensor("ps_all", [128, 8, 512], F32))
        banks = [ps_all[:, i, :] for i in range(8)]
        BR = [Reg(f"bank{i}", excl=True) for i in range(8)]
        AR = Arena(ar_t, NAR)

        XR = [[Reg(f"x{n}_{t}") for t in range(4)] for n in range(8)]
        SCR = [Reg(f"scr{i}") for i in range(8)]
        CST = Reg("cst")
        CB = Reg("cb")
        LAM = Reg("lam")
        OUT = Reg("out")

        ones_bf = cb[:, 0:128]
        bd_bf = cb[:, 128:256]
        maskv = cb[:, 256:256 + 1024].rearrange("p (m q) -> p m q", m=8)
        tri_bf = cb[:, 1280:1408]
        ident_bf = cb[:, 1408:1536]
        negtri_bf = cb[:, 1536:1664]

        def cs(col):
            return cst[:, col:col + 1]

        def mm(out, lhsT, rhs, start, stop, reads, writes, inc, skip=False):
            if skip:
                P.op("pe", lambda e: e.matmul(out, lhsT=lhsT, rhs=rhs, start=start, stop=stop,
                                              skip_group_check=True), reads, writes, inc)
            else:
                P.op("pe", lambda e: e.matmul(out, lhsT=lhsT, rhs=rhs, start=start, stop=stop),
                     reads, writes, inc)

        def act(out, in_, func, reads, writes, bias=None, scale=None):
            kw = {}
            if bias is not None:
                kw["bias"] = bias
            if scale is not None:
                kw["scale"] = scale
            P.op("act", lambda e: e.activation(out=out, in_=in_, func=func, **kw), reads, writes)

        def tt(eng, out, in0, in1, op, reads, writes):
            P.op(eng, lambda e: e.tensor_tensor(out=out, in0=in0, in1=in1, op=op), reads, writes)

        def stt(eng, out, in0, scalar, in1, op0, op1, reads, writes):
            P.op(eng, lambda e: e.scalar_tensor_tensor(out=out, in0=in0, scalar=scalar, in1=in1,
                                                       op0=op0, op1=op1), reads, writes)

        def recip(out, in_, reads, writes):
            P.op("dve", lambda e: e.reciprocal(out=out, in_=in_), reads, writes)

        def rsqrt_act(dst, src, scale, reads, dreg):
            act(dst, src, AF.Ln, list(reads) + [CST], [dreg], bias=cs(C_EPS), scale=scale)
            act(dst, dst, AF.Exp, [dreg], [dreg], scale=-0.5)

        def recip_act(dst, src, reads, dreg):
            act(dst, src, AF.Ln, list(reads), [dreg])
            act(dst, dst, AF.Exp, [dreg], [dreg], scale=-1.0)

        def pipeline(units, depth=1):
            n = len(units)
            for i in range(min(depth, n)):
                units[i][0]()
            for i in range(n):
                if i + depth < n:
                    units[i + depth][0]()
                units[i][1]()

        def wdma(sem, out, in_, reads, writes):
            P.dma("pool", sem, lambda e: e.dma_start(out=out, in_=in_), reads, writes)

        P.dma("sp", "cst", lambda e: e.dma_start(out=cst[:, 0:NCST], in_=cst_d), writes=[CST])
        wdma("cb", cb[:], cb_d, [], [CB])
        xT_dv = xT_d.rearrange("n p s -> p n s")
        for t in range(4):
            for hf in range(2):
                P.dma("sp", f"x{t * 2 + hf}",
                      (lambda t, hf: lambda e: e.dma_start(out=xT[:, 4 * hf:4 * hf + 4, 512 * t:512 * t + 512],
                                                           in_=xT_dv[:, 4 * hf:4 * hf + 4, 512 * t:512 * t + 512]))(t, hf),
                      writes=[XR[n][t] for n in range(4 * hf, 4 * hf + 4)])

        hT_ap, _ = AR.region("hT", 0, 16384)
        hT = hT_ap.rearrange("p (k s) -> p k s", k=8)
        HR = [[Reg(f"h{k}_{t}") for t in range(4)] for k in range(8)]
        BASE = 16384

        def tsl(t):
            return slice(512 * t, 512 * t + 512)

        nsq_t = sb("nsq", [128, 2, 512], BF16)
        NSQR = [Reg("nsq0"), Reg("nsq1")]

        def norm_stats(gcol, t):
            sb_i = 6 + (t % 2)
            for k in range(8):
                i = k % 2
                act(nsq_t[:, i, :], xT[:, k, tsl(t)], AF.Square, [XR[k][t]], [NSQR[i]])
                mm(banks[sb_i][:], ones_bf, nsq_t[:, i, :], k == 0, k == 7, [CB, NSQR[i]], [BR[sb_i]], True)
            si = t % 2
            rsqrt_act(scr[:, si, :], banks[sb_i][:], 1.0 / D, [BR[sb_i]], SCR[si])

        def norm_write(gcol, t, k):
            si = t % 2
            stt("dve", hT[:, k, tsl(t)], xT[:, k, tsl(t)], cs(gcol + k), scr[:, si, :],
                ALU.mult, ALU.mult, [XR[k][t], SCR[si], CST], [HR[k][t]])

        def norm_tile(gcol, t):
            norm_stats(gcol, t)
            for k in range(8):
                norm_write(gcol, t, k)

        def rmsnorm(gcol):
            for t in range(4):
                norm_tile(gcol, t)

        def ffn(l, f, do_norm=True, next_g=None):
            if do_norm:
                rmsnorm(C_NORM + (l * 3 + (0 if f == 0 else 2)) * 8)
            tbase = (l * 2 + f) * NFC
            act_ap, act_rs = AR.multi("act", BASE, 12288, 24)
            actv = act_ap.rearrange("p (c s) -> p c s", c=6)
            ACTR = [[act_rs[c * 4 + t] for t in range(4)] for c in range(6)]
            win, winr = [], []
            for i in range(3):
                a, r = AR.region(f"win{i}", BASE + 28672 + 2048 * i, 2048)
                win.append(a.rearrange("p (k c) -> p k c", k=8))
                winr.append(r)
            wout, woutr = [], []
            for i in range(12):
                a, r = AR.region(f"wout{i}", BASE + 12288 + 1024 * i, 1024)
                wout.append(a)
                woutr.append(r)

            x0 = [XR[n][0] for n in range(8)] if (l == 0 and f == 0) else []

            def load_in(c):
                wdma(f"win{c % 3}", win[c % 3], f_wi[tbase + c].rearrange("p (k c) -> p k c", k=8),
                     x0 if c in (1, 2) else [], [winr[c % 3]])

            def load_out(c):
                wdma(f"wout{c % 12}", wout[c % 12], f_wo[tbase + c], x0 if c < 3 else [], [woutr[c % 12]])

            def load(c):
                load_in(c)
                load_out(c)

            for c in range(3):
                load_in(c)
            for c in range(3):
                load_out(c)
            pair = 0
            ybank = 0
            for grp in GROUPS:
                for ci, c in enumerate(grp):
                    w = win[c % 3]
                    wr = winr[c % 3]
                    for t in range(4):
                        bg, bu = (0, 1) if pair % 2 == 0 else (2, 3)
                        pair += 1
                        for k in range(8):
                            mm(banks[bg][:], w[:, k, 0:128], hT[:, k, tsl(t)], k == 0, k == 7,
                               [wr, HR[k][t]], [BR[bg]], k == 7)
                        for k in range(8):
                            mm(banks[bu][:], w[:, k, 128:256], hT[:, k, tsl(t)], k == 0, k == 7,
                               [wr, HR[k][t]], [BR[bu]], k == 7)
                        si = 2 + (pair % 2)
                        act(scr[:, si, :], banks[bg][:], AF.Silu, [BR[bg]], [SCR[si]])
                        tt("dve", actv[:, ci, tsl(t)], scr[:, si, :], banks[bu][:], ALU.mult,
                           [SCR[si], BR[bu]], [ACTR[ci][t]])
                    if c + 3 < NFC:
                        load(c + 3)
                for t in range(4):
                    for n in range(8):
                        yb = 4 + (ybank % 2)
                        ybank += 1
                        for ci, c in enumerate(grp):
                            mm(banks[yb][:], wout[c % 12][:, n * 128:(n + 1) * 128], actv[:, ci, tsl(t)],
                               ci == 0, ci == len(grp) - 1, [woutr[c % 12], ACTR[ci][t]], [BR[yb]],
                               ci == len(grp) - 1)
                        stt("dve", xT[:, n, tsl(t)], banks[yb][:], 0.5, xT[:, n, tsl(t)], ALU.mult, ALU.add,
                            [BR[yb], XR[n][t]], [XR[n][t]])
                        if next_g is not None and grp is GROUPS[-1] and t > 0:
                            norm_write(next_g, t - 1, n)
                    if next_g is not None and grp is GROUPS[-1]:
                        norm_stats(next_g, t)
                        if t == 3:
                            for k in range(8):
                                norm_write(next_g, 3, k)

        class WRing:
            def __init__(self, name, off, n, cols):
                self.name = name
                self.n = n
                self.ap, self.reg = [], []
                for i in range(n):
                    a, r = AR.region(f"{name}{i}", off + cols * i, cols)
                    self.ap.append(a)
                    self.reg.append(r)
                self.i = 0

            def load(self, src):
                i = self.i % self.n
                self.i += 1
                wdma(f"{self.name}{i}", self.ap[i], src, [], [self.reg[i]])
                return self.ap[i], self.reg[i]

            def set_queue(self, srcs):
                self.q = list(srcs)
                self.issued = 0
                self.taken = 0
                self.slots = []

            def prime(self):
                while self.issued < len(self.q) and self.issued < self.taken + self.n:
                    self.slots.append(self.load(self.q[self.issued]))
                    self.issued += 1

            def nxt(self):
                self.prime()
                r = self.slots[self.taken]
                self.taken += 1
                return r

        def qk_project(w_ap, w_reg, gcol, dst, dst_reg, dil, sq_ap, sqr, rawbanks=(0, 1, 2, 3)):
            wv = w_ap.rearrange("p (k c) -> p k c", k=8)
            units = []
            for t in range(4):
                def head(t=t):
                    rb = rawbanks[t % len(rawbanks)]
                    for k in range(8):
                        mm(banks[rb][:], wv[:, k, :], hT[:, k, tsl(t)], k == 0, k == 7,
                           [w_reg, HR[k][t]], [BR[rb]], k == 7)
                    i = t % 2
                    act(sq_ap[i], banks[rb][:], AF.Square, [BR[rb]], [sqr[i]])

                def tail(t=t):
                    rb = rawbanks[t % len(rawbanks)]
                    i = t % 2
                    sbk = 6 + (t % 2)
                    mm(banks[sbk][:], bd_bf, sq_ap[i], True, True, [CB, sqr[i]], [BR[sbk]], True)
                    si = 6 + (t % 2)
                    rsqrt_act(scr[:, si, :], banks[sbk][:], 1.0 / HD, [BR[sbk]], SCR[si])
                    if dil == 1:
                        o = dst[:, tsl(t)]
                        i0 = banks[rb][:]
                        i1 = scr[:, si, :]
                    else:
                        n_per = 512 // dil
                        o = dst.rearrange("p (c i) -> p c i", c=dil)[:, :, n_per * t:n_per * (t + 1)]
                        i0 = banks[rb][:].rearrange("p (i c) -> p c i", c=dil)
                        i1 = scr[:, si, :].rearrange("p (i c) -> p c i", c=dil)
                    stt("dve", o, i0, cs(gcol), i1, ALU.mult, ALU.mult, [BR[rb], SCR[si], CST], [dst_reg])
                units.append((head, tail))
            pipeline(units)

        def dilated(l, odil, ODR, mw):
            off = BASE + 4096
            qk_ap, qk_reg = {}, {}
            for g in range(3):
                for which in range(2):
                    a, r = AR.region(f"dqk{g}{which}", off, 2048)
                    qk_ap[(g, which)] = a
                    qk_reg[(g, which)] = r
                    off += 2048
            v_ap, v_reg = {}, {}
            for g in range(3):
                a, r = AR.region(f"dv{g}", off, 4096)
                v_ap[g] = a.rearrange("p (b h e) -> p b h e", b=16, h=2)
                v_reg[g] = r
                off += 4096
            ring = mw
            e_ap, e_reg = [], []
            for i in range(6):
                a, r = AR.region(f"dE{i}", off, 512)
                e_ap.append(a)
                e_reg.append(r)
                off += 512
            sq_ap, sqr = [], []
            for i in range(2):
                a, r = AR.region(f"dsq{i}", off, 512)
                sq_ap.append(a)
                sqr.append(r)
                off += 512
            assert off <= NAR - 4096
            ecnt = [0]

            for jp in range(2):
                for g in range(3):
                    if jp != 0:
                        break
                    P.op("pool", (lambda g: lambda e: e.memset(v_ap[g][:, :, 0, 64:128], 1.0))(g), [], [v_reg[g]])
                    P.op("pool", (lambda g: lambda e: e.memset(v_ap[g][:, :, 1, 0:64], 1.0))(g), [], [v_reg[g]])
                for g in range(3):
                    dil = DILS[g]
                    for which in range(2):
                        w_ap, w_reg = ring.nxt()
                        gcol = C_QKDIL + ((l * 2 + which) * 3 + g) * 2 + jp
                        qk_project(w_ap, w_reg, gcol, qk_ap[(g, which)], qk_reg[(g, which)], dil, sq_ap, sqr)
                    w_ap, w_reg = ring.nxt()
                    wv = w_ap.rearrange("p (k c) -> p k c", k=8)
                    for b4 in range(4):
                        vb = 4 + (b4 % 2)
                        for bb in range(4):
                            blk = b4 * 4 + bb
                            nblk = 16 // dil
                            c, n = blk // nblk, blk % nblk
                            start_tok = 128 * n * dil + c
                            for k in range(8):
                                lhsT = hT[:, k, start_tok:start_tok + 127 * dil + 1:dil] if dil > 1 else \
                                    hT[:, k, start_tok:start_tok + 128]
                                mm(banks[vb][:, bb * 128:(bb + 1) * 128], lhsT, wv[:, k, :], k == 0, k == 7,
                                   [w_reg] + [HR[k][tt_] for tt_ in range(4)], [BR[vb]],
                                   (k == 7 and bb == 3))
                        src = banks[vb][:].rearrange("p (b h e) -> p b h e", b=4, h=2)
                        P.op("act", (lambda g, b4, src: lambda e: e.activation(
                            out=v_ap[g][:, 4 * b4:4 * b4 + 4, 0, 0:64], in_=src[:, :, 0, :], func=AF.Copy))(g, b4, src),
                            [BR[vb]], [v_reg[g]])
                        P.op("act", (lambda g, b4, src: lambda e: e.activation(
                            out=v_ap[g][:, 4 * b4:4 * b4 + 4, 1, 64:128], in_=src[:, :, 1, :], func=AF.Copy))(g, b4, src),
                            [BR[vb]], [v_reg[g]])
                units = []
                for hl in range(2):
                    j = 2 * jp + hl
                    ps = slice(64 * hl, 64 * hl + 64)
                    num = slice(0, 64) if hl == 0 else slice(64, 128)
                    den = slice(64, 128) if hl == 0 else slice(0, 64)
                    for n in range(4):
                        ob = 4 + (n % 2)
                        jobs = []
                        for which, mid in ((0, 0), (1, 1)):
                            items = []
                            for qb in range(4):
                                b = 4 * n + qb
                                kb = b - 1 if which == 0 else b
                                if kb < 0:
                                    continue
                                items.append((qk_ap[(0, 1)][ps, 128 * kb:128 * kb + 128],
                                              qk_ap[(0, 0)][ps, 128 * b:128 * b + 128],
                                              kb, banks[ob][:, 128 * qb:128 * qb + 128], 128, 128 * qb))
                            jobs.append((0, mid, items))
                        for which, mid in ((0, 0), (1, 1)):
                            items = []
                            for c in range(4):
                                kn = n - 1 if which == 0 else n
                                if kn < 0:
                                    continue
                                items.append((qk_ap[(1, 1)][ps, 512 * c + 128 * kn:512 * c + 128 * kn + 128],
                                              qk_ap[(1, 0)][ps, 512 * c + 128 * n:512 * c + 128 * n + 128],
                                              4 * c + kn, banks[ob][:, c:512:4], 128, 128 * c))
                            jobs.append((1, mid, items))
                        items = []
                        for c in range(16):
                            items.append((qk_ap[(2, 1)][ps, 128 * c:128 * c + 128],
                                          qk_ap[(2, 0)][ps, 128 * c + 32 * n:128 * c + 32 * n + 32],
                                          c, banks[ob][:, c:512:16], 32, 32 * c))
                        jobs.append((2, 2, items))
                        jobs = [jb for jb in jobs if jb[2]]
                        for ji, (g, mid, items) in enumerate(jobs):
                            sbk = ecnt[0] % 4
                            ei = ecnt[0] % 6
                            ecnt[0] += 1
                            first = ji == 0
                            last = ji == len(jobs) - 1

                            def head(g=g, items=items, sbk=sbk, first=first, ob=ob):
                                if first:
                                    P.op("dve", lambda e: e.memset(banks[ob][:], 0.0), [], [BR[ob]])
                                for ii, (k_ap, q_ap, vblk, dest, ncols, col0) in enumerate(items):
                                    mm(banks[sbk][:, col0:col0 + ncols], k_ap, q_ap, True, True,
                                       [qk_reg[(g, 1)], qk_reg[(g, 0)]], [BR[sbk]], ii == len(items) - 1, skip=True)

                            def tail(g=g, mid=mid, items=items, sbk=sbk, ei=ei, first=first, last=last,
                                     ob=ob, n=n, j=j, hl=hl, num=num, den=den):
                                lo = items[0][5]
                                hi = items[-1][5] + items[-1][4]
                                act(e_ap[ei][:, lo:hi], banks[sbk][:, lo:hi], AF.Exp, [BR[sbk]], [e_reg[ei]],
                                    scale=0.125)
                                if mid == 2:
                                    mk = maskv[:, 2 * j + 1:2 * j + 2, 32 * n:32 * n + 32].broadcast_to([128, 16, 32])
                                    ev = e_ap[ei][:, :].rearrange("p (c q) -> p c q", c=16)
                                else:
                                    nb = (hi - lo) // 128
                                    mk = maskv[:, 2 * j + mid:2 * j + mid + 1, :].broadcast_to([128, nb, 128])
                                    ev = e_ap[ei][:, lo:hi].rearrange("p (c q) -> p c q", c=nb)
                                tt("dve", ev, ev, mk, ALU.mult, [e_reg[ei], CB], [e_reg[ei]])
                                for ii, (k_ap, q_ap, vblk, dest, ncols, col0) in enumerate(items):
                                    mm(dest, v_ap[g][:, vblk, hl, :], e_ap[ei][:, col0:col0 + ncols], False, False,
                                       [v_reg[g], e_reg[ei]], [BR[ob]], ii == len(items) - 1, skip=True)
                                if deferred:
                                    deferred.pop(0)()
                                if last:
                                    def fin_piece(ob=ob, n=n, num=num, den=den):
                                        si = 4 + (n % 2)
                                        recip_act(scr[num, si, :], banks[ob][den, :], [BR[ob]], SCR[si])
                                        tt("dve", odil[num, jp, tsl(n)], banks[ob][num, :], scr[num, si, :], ALU.mult,
                                           [BR[ob], SCR[si]], [ODR[jp]])
                                    deferred.append(fin_piece)
                            units.append((head, tail))
                deferred = []
                pipeline(units, depth=2)
                while deferred:
                    deferred.pop(0)()

        def diffattn(l, odiff, OFR, mw, before_last_attn=None):
            off = BASE + 4096 + 8192
            q_ap, q_reg, k_ap, k_reg, v_ap, v_reg = [], [], [], [], [], []
            for i in range(2):
                a, r = AR.region(f"fq{i}", off, 2048)
                q_ap.append(a)
                q_reg.append(r)
                off += 2048
                a, r = AR.region(f"fk{i}", off, 2048)
                k_ap.append(a)
                k_reg.append(r)
                off += 2048
                a, r = AR.region(f"fv{i}", off, 2048)
                v_ap.append(a.rearrange("p (b e) -> p b e", b=16))
                v_reg.append(r)
                off += 2048
            ring = mw
            ep_ap, ep_reg = [], []
            for i in range(3):
                a, r = AR.region(f"fEp{i}", off, 1024)
                ep_ap.append(a.rearrange("p (m q) -> p m q", m=2))
                ep_reg.append(r)
                off += 1024
            sq_ap, sqr = [], []
            for i in range(2):
                a, r = AR.region(f"fsq{i}", off, 512)
                sq_ap.append(a)
                sqr.append(r)
                off += 512
            assert off <= NAR
            P.op("dve", lambda e: e.tensor_tensor(out=lamprod[:, 0:128],
                                                  in0=cst[:, C_LQ + l * 128:C_LQ + l * 128 + 128],
                                                  in1=cst[:, C_LK + l * 128:C_LK + l * 128 + 128], op=ALU.mult),
                 [CST], [LAM])
            P.op("dve", lambda e: e.tensor_reduce(out=lamtmp[:, 0:2],
                                                  in_=lamprod[:, 0:128].rearrange("p (a b) -> p a b", a=2),
                                                  axis=mybir.AxisListType.X, op=ALU.add), [LAM], [LAM])
            act(lamtmp[:, 2:4], lamtmp[:, 0:2], AF.Exp, [LAM], [LAM])
            tt("dve", lamtmp[:, 4:5], lamtmp[:, 3:4], lamtmp[:, 2:3], ALU.subtract, [LAM], [LAM])
            tt("dve", cst[:, C_NLAM + l:C_NLAM + l + 1], lamtmp[:, 4:5], cs(C_LAMI + l), ALU.subtract,
               [LAM, CST], [CST])
            P.op("dve", lambda e: e.tensor_scalar(out=cst[:, C_SUBS + 4 * l:C_SUBS + 4 * l + 4],
                                                  in0=cst[:, C_SUB + 4 * l:C_SUB + 4 * l + 4],
                                                  scalar1=cs(C_OML + l), scalar2=None, op0=ALU.mult),
                 [CST], [CST])

            cnt = [0]
            ecnt = [0]
            deferred = []
            pre = []
            for h in range(4):
                i = h % 2
                w_ap, w_reg = ring.nxt()
                qk_project(w_ap, w_reg, C_QKDIFF + (l * 2 + 0) * 4 + h, q_ap[i], q_reg[i], 1, sq_ap, sqr)
                for _ in range(2):
                    if deferred:
                        deferred.pop(0)(7)
                w_ap, w_reg = ring.nxt()
                qk_project(w_ap, w_reg, C_QKDIFF + (l * 2 + 1) * 4 + h, k_ap[i], k_reg[i], 1, sq_ap, sqr)
                while deferred:
                    deferred.pop(0)(7)
                w_ap, w_reg = ring.nxt()
                wv = w_ap.rearrange("p (k c) -> p k c", k=8)
                for b4 in range(4):
                    vb = 2 + (b4 % 2)
                    for bb in range(4):
                        blk = b4 * 4 + bb
                        for k in range(8):
                            mm(banks[vb][:, bb * 128:(bb + 1) * 128], hT[:, k, 128 * blk:128 * blk + 128],
                               wv[:, k, :], k == 0, k == 7, [w_reg, HR[k][blk // 4]], [BR[vb]],
                               (k == 7 and bb == 3))
                    src = banks[vb][:].rearrange("p (b e) -> p b e", b=4)
                    P.op("act", (lambda i, b4, src: lambda e: e.activation(
                        out=v_ap[i][:, 4 * b4:4 * b4 + 4, :], in_=src, func=AF.Copy))(i, b4, src),
                        [BR[vb]], [v_reg[i]])
                if h == 3 and before_last_attn is not None:
                    before_last_attn()
                units = []
                SBP = (0, 6)
                tri2 = tri_bf.unsqueeze(1).broadcast_to([128, 2, 128])
                for t in range(4):
                    nkb = 4 * t + 4
                    for kb in range(nkb):
                        dk = kb - 4 * t
                        qlo = 0 if dk < 0 else 128 * dk
                        fin = kb == nkb - 1
                        slot = {}

                        def head(kb=kb, qlo=qlo, slot=slot, t=t, qlo_mask=(dk >= 0)):
                            sb0 = slot["sb0"] = SBP[cnt[0] % 2]
                            cnt[0] += 1
                            for m in range(2):
                                ps = slice(64 * m, 64 * m + 64)
                                mm(banks[sb0 + m][:, qlo:512], k_ap[i][ps, 128 * kb:128 * kb + 128],
                                   q_ap[i][ps, 512 * t + qlo:512 * t + 512], True, True,
                                   [k_reg[i], q_reg[i]], [BR[sb0 + m]], True, skip=True)
                            if qlo_mask:
                                for m in range(2):
                                    mm(banks[sb0 + m][:, qlo:qlo + 128], ident_bf, negtri_bf, False, True,
                                       [CB], [BR[sb0 + m]], True, skip=True)

                        def tail(kb=kb, dk=dk, qlo=qlo, slot=slot, t=t, nkb=nkb, fin=fin):
                            sb0 = slot["sb0"]
                            ei = ecnt[0] % 3
                            ecnt[0] += 1
                            ep = ep_ap[ei]
                            act(ep[:, :, qlo:512], ps_all[:, sb0:sb0 + 2, qlo:512], AF.Exp,
                                [BR[sb0], BR[sb0 + 1], CST], [ep_reg[ei]],
                                bias=cs(C_ALIBI + h * 19 + (dk + 15)), scale=0.125)
                            had_pre = bool(pre)
                            while pre:
                                pre.pop(0)()
                            order = ((2, 0), (2, 1), (4, 0), (4, 1)) if had_pre else ((4, 0), (2, 0), (4, 1), (2, 1))
                            for (bb, m) in order:
                                if bb == 4:
                                    mm(banks[4 + m][:, qlo:512], ones_bf, ep[:, m, qlo:512], kb == 0, kb == nkb - 1,
                                       [CB, ep_reg[ei]], [BR[4 + m]], True, skip=True)
                                else:
                                    mm(banks[2 + m][:, qlo:512], v_ap[i][:, kb, :], ep[:, m, qlo:512], kb == 0,
                                       kb == nkb - 1, [v_reg[i], ep_reg[ei]], [BR[2 + m]], True, skip=True)
                            if deferred:
                                deferred.pop(0)(sb0)
                            if not fin:
                                return
                            for m in range(2):
                                P.op("dve", (lambda m: lambda e: e.tensor_copy(out=scr[:, m, :], in_=banks[2 + m][:]))(m),
                                     [BR[2 + m]], [SCR[m]])
                                pre.append((lambda m: lambda: act(scr[:, 2 + m, :], banks[4 + m][:], AF.Ln, [BR[4 + m]],
                                                                  [SCR[2 + m]]))(m))

                            def piece_b1(fb):
                                act(scr[:, 2, :], scr[:, 2, :], AF.Exp, [SCR[2]], [SCR[2]], scale=-1.0)
                                tt("dve", scr[:, 0, :], scr[:, 0, :], scr[:, 2, :], ALU.mult, [SCR[0], SCR[2]], [SCR[0]])

                            def piece_b2(fb):
                                act(scr[:, 3, :], scr[:, 3, :], AF.Exp, [SCR[3]], [SCR[3]], scale=-1.0)
                                tt("dve", scr[:, 1, :], scr[:, 1, :], scr[:, 3, :], ALU.mult, [SCR[1], SCR[3]], [SCR[1]])
                                stt("dve", scr[:, 0, :], scr[:, 1, :], cs(C_NLAM + l), scr[:, 0, :], ALU.mult, ALU.add,
                                    [SCR[0], SCR[1], CST], [SCR[0]])

                            def piece_c1(fb, t=t):
                                si = t % 2
                                tt("dve", sq_ap[si], scr[:, 0, :], scr[:, 0, :], ALU.mult, [SCR[0]], [sqr[si]])
                                mm(banks[fb][:], ones_bf, sq_ap[si], True, True, [CB, sqr[si]], [BR[fb]], True)
                                act(scr[:, 2, :], banks[fb][:], AF.Ln, [BR[fb], CST], [SCR[2]], bias=cs(C_EPS), scale=1.0 / 128)

                            def piece_c2(fb, t=t, h=h):
                                act(scr[:, 2, :], scr[:, 2, :], AF.Exp, [SCR[2]], [SCR[2]], scale=-0.5)
                                stt("dve", odiff[:, h, tsl(t)], scr[:, 0, :], cs(C_SUBS + 4 * l + h), scr[:, 2, :],
                                    ALU.mult, ALU.mult, [SCR[0], SCR[2], CST], [OFR[h]])
                            deferred.extend([piece_b1, piece_b2, piece_c1, piece_c2])
                        units.append((head, tail))
                pipeline(units, depth=1)
                while pre:
                    pre.pop(0)()
                if h == 3:
                    while deferred:
                        deferred.pop(0)(0)

        def final(l, odil, ODR, odiff, OFR, ring, wo_ap, wo_regs, next_g=None):
            off = BASE + 4096 + 8192
            y_ap, y_rs = AR.multi("yT", off, 16384, 32)
            yT = y_ap.rearrange("p (m s) -> p m s", m=8)
            YR = [[y_rs[m * 4 + t] for t in range(4)] for m in range(8)]
            wov = wo_ap.rearrange("p (m c) -> p m c", m=8)
            for m in range(8):
                w_ap, w_reg = ring.nxt()
                ga = w_ap[:, 0:1024].rearrange("p (k c) -> p k c", k=8)
                gb = w_ap[:, 1024:2048].rearrange("p (k c) -> p k c", k=8)
                wa = w_ap[:, 2048:2304].rearrange("p (j c) -> p j c", j=2)
                wb = w_ap[:, 2560:3072].rearrange("p (h c) -> p h c", h=4)
                for t in range(4):
                    b0 = 0 if t % 2 == 0 else 4
                    for k in range(8):
                        mm(banks[b0][:], ga[:, k, :], hT[:, k, tsl(t)], k == 0, k == 7, [w_reg, HR[k][t]],
                           [BR[b0]], k == 7)
                    for k in range(8):
                        mm(banks[b0 + 1][:], gb[:, k, :], hT[:, k, tsl(t)], k == 0, k == 7, [w_reg, HR[k][t]],
                           [BR[b0 + 1]], k == 7)
                    for jp in range(2):
                        mm(banks[b0 + 2][:], wa[:, jp, :], odil[:, jp, tsl(t)], jp == 0, jp == 1,
                           [w_reg, ODR[jp]], [BR[b0 + 2]], jp == 1)
                    for h in range(4):
                        mm(banks[b0 + 3][:], wb[:, h, :], odiff[:, h, tsl(t)], h == 0, h == 3,
                           [w_reg, OFR[h]], [BR[b0 + 3]], h == 3)
                    s0 = 0 if t % 2 == 0 else 2
                    act(scr[:, s0, :], banks[b0][:], AF.Sigmoid, [BR[b0]], [SCR[s0]])
                    act(scr[:, s0 + 1, :], banks[b0 + 1][:], AF.Sigmoid, [BR[b0 + 1]], [SCR[s0 + 1]])
                    tt("dve", scr[:, s0, :], scr[:, s0, :], banks[b0 + 2][:], ALU.mult, [SCR[s0], BR[b0 + 2]], [SCR[s0]])
                    tt("dve", scr[:, s0 + 1, :], scr[:, s0 + 1, :], banks[b0 + 3][:], ALU.mult,
                       [SCR[s0 + 1], BR[b0 + 3]], [SCR[s0 + 1]])
                    tt("dve", yT[:, m, tsl(t)], scr[:, s0, :], scr[:, s0 + 1, :], ALU.add,
                       [SCR[s0], SCR[s0 + 1]], [YR[m][t]])
            ob = 0
            for t in range(4):
                for n in range(8):
                    b = ob % 8
                    ob += 1
                    for m in range(8):
                        mm(banks[b][:], wov[:, m, n * 128:(n + 1) * 128], yT[:, m, tsl(t)], m == 0, m == 7,
                           [wo_regs[m // 4], YR[m][t]], [BR[b]], m == 7)
                    tt("dve", xT[:, n, tsl(t)], banks[b][:], xT[:, n, tsl(t)], ALU.add, [BR[b], XR[n][t]], [XR[n][t]])
                    if next_g is not None and t > 0:
                        norm_write(next_g, t - 1, n)
                if next_g is not None:
                    norm_stats(next_g, t)
                    if t == 3:
                        for k in range(8):
                            norm_write(next_g, 3, k)

        def mixer(l, do_norm=True, next_g=None):
            if do_norm:
                rmsnorm(C_NORM + (l * 3 + 1) * 8)
            od_ap, ODR = AR.multi("odil", BASE, 4096, 2)
            odil = od_ap.rearrange("p (j s) -> p j s", j=2)
            of_ap, OFR = AR.multi("odiff", BASE + 4096, 8192, 4)
            odiff = of_ap.rearrange("p (h s) -> p h s", h=4)
            mw = WRing("mw", NAR - 4096, 4, 1024)
            tiles = []
            if "dil" in mix_parts:
                for jp in range(2):
                    for g in range(3):
                        tiles += [g * 2 + jp, 6 + g * 2 + jp, 12 + g * 2 + jp]
            if "diff" in mix_parts:
                for h in range(4):
                    tiles += [18 + h, 22 + h, 26 + h]
            mw.set_queue([m_wi[l * 30 + t_] for t_ in tiles])
            mw.prime()
            if "dil" in mix_parts:
                dilated(l, odil, ODR, mw)
            fin = {}

            def prefetch_final():
                fin["ring"] = WRing("nw", BASE + 4096 + 8192 + 16384, 2, 3072)
                fin["ring"].set_queue([m_fin[l * 8 + m] for m in range(8)])
                fin["ring"].prime()
                off = BASE + 4096 + 8192 + 16384 + 6144
                a0, r0 = AR.region("wo0", off, 4096)
                a1, r1 = AR.region("wo1", off + 4096, 4096)
                fin["wo_ap"] = ar_t[:, off:off + 8192]
                fin["wo_regs"] = [r0, r1]
                wdma("wo_a", a0, m_wo[l][:, 0:4096], [], [r0])
                wdma("wo_b", a1, m_wo[l][:, 4096:8192], [], [r1])

            if "diff" in mix_parts:
                diffattn(l, odiff, OFR, mw, prefetch_final if "fin" in mix_parts else None)
            if "fin" in mix_parts:
                if not fin:
                    prefetch_final()
                final(l, odil, ODR, odiff, OFR, fin["ring"], fin["wo_ap"], fin["wo_regs"], next_g)

        plan = []
        for l in range(n_layers):
            for st_name in ("ffn1", "mix", "ffn2"):
                if st_name in stages:
                    plan.append((st_name, l))
        fuse_norm = "fin" in mix_parts

        def gcol_of(st_name, l):
            return C_NORM + (l * 3 + {"ffn1": 0, "mix": 1, "ffn2": 2}[st_name]) * 8

        for i, (st_name, l) in enumerate(plan):
            do_norm = (i == 0) or not fuse_norm
            after = None
            if fuse_norm and i + 1 < len(plan):
                after = gcol_of(*plan[i + 1])
            if st_name == "ffn1":
                ffn(l, 0, do_norm, after)
            elif st_name == "mix":
                mixer(l, do_norm, after)
            else:
                ffn(l, 1, do_norm, after)

        out_dv = out_d.rearrange("n p s -> p n s")
        for t in range(4):
            for hf in range(2):
                P.dma("sp", "out",
                      (lambda t, hf: lambda e: e.dma_start(out=out_dv[:, 4 * hf:4 * hf + 4, 512 * t:512 * t + 512],
                                                           in_=xT[:, 4 * hf:4 * hf + 4, 512 * t:512 * t + 512]))(t, hf),
                      reads=[XR[n][t] for n in range(4 * hf, 4 * hf + 4)], writes=[OUT])
        P.wait_all("sp", [OUT])
        P.emit(nc, st)
    return nc


def _prep_weights(inp):
    f32 = np.float32
    f_wi = np.empty((L * 2 * NFC, 128, 2048), f32)
    f_wo = np.empty((L * 2 * NFC, 128, 1024), f32)
    for l in range(L):
        for f, (wi, wo) in enumerate(((inp["ffn1_w_in"], inp["ffn1_w_out"]), (inp["ffn2_w_in"], inp["ffn2_w_out"]))):
            w = np.asarray(wi[l], f32)
            g = w[:, :FF].reshape(8, 128, NFC, 128)
            u = w[:, FF:].reshape(8, 128, NFC, 128)
            t = np.stack([g, u], axis=3)
            t = t.transpose(2, 1, 0, 3, 4).reshape(NFC, 128, 2048)
            base = (l * 2 + f) * NFC
            f_wi[base:base + NFC] = t
            f_wo[base:base + NFC] = np.asarray(wo[l], f32).reshape(NFC, 128, 1024)
    m_wi = np.empty((L * 30, 128, 1024), f32)
    m_fin = np.zeros((L * 8, 128, 3072), f32)
    m_wo = np.empty((L, 128, 8192), f32)
    for l in range(L):
        w = np.asarray(inp["w_in"][l], f32)
        tiles = w.reshape(8, 128, 46, 128).transpose(2, 1, 0, 3).reshape(46, 128, 1024)
        m_wi[l * 30:(l + 1) * 30] = tiles[:30]
        wa = np.asarray(inp["w_branch_dil"][l], f32)
        wb = np.asarray(inp["w_branch_diff"][l], f32)
        for m in range(8):
            m_fin[l * 8 + m, :, 0:1024] = tiles[30 + m]
            m_fin[l * 8 + m, :, 1024:2048] = tiles[38 + m]
            a = wa[:, m * 128:(m + 1) * 128].reshape(2, 128, 128).transpose(1, 0, 2).reshape(128, 256)
            m_fin[l * 8 + m, :, 2048:2304] = a
            b = wb[:, m * 128:(m + 1) * 128].reshape(4, 128, 128).transpose(1, 0, 2).reshape(128, 512)
            m_fin[l * 8 + m, :, 2560:3072] = b
        m_wo[l] = np.asarray(inp["w_out"][l], f32).reshape(8, 128, 1024).transpose(1, 0, 2).reshape(128, 8192)
    return f_wi, f_wo, m_wi, m_fin, m_wo


def _prep_consts(inp):
    f32 = np.float32
    cst = np.zeros((128, NCST), f32)
    norms = (inp["ffn1_norm"], inp["mix_norm"], inp["ffn2_norm"])
    for l in range(L):
        for w in range(3):
            cst[:, C_NORM + (l * 3 + w) * 8:C_NORM + (l * 3 + w) * 8 + 8] = \
                np.asarray(norms[w][l], f32).reshape(8, 128).T
        gd = np.asarray(inp["qk_gain_dil"][l], f32)
        for qk in range(2):
            for g in range(3):
                for jp in range(2):
                    cst[:, C_QKDIL + ((l * 2 + qk) * 3 + g) * 2 + jp] = gd[qk, g, 2 * jp:2 * jp + 2].reshape(128)
        gf = np.asarray(inp["qk_gain_diff"][l], f32)
        for qk in range(2):
            for h in range(4):
                cst[:, C_QKDIFF + (l * 2 + qk) * 4 + h] = gf[qk, h].reshape(128)
        cst[:, C_SUB + 4 * l:C_SUB + 4 * l + 4] = np.asarray(inp["diff_subnorm"][l], f32).T
        lam_init = 0.8 - 0.6 * math.exp(-0.3 * l)
        cst[:, C_LAMI + l] = lam_init
        cst[:, C_OML + l] = 1.0 - lam_init
        cst[:, C_LQ + l * 128:C_LQ + (l + 1) * 128] = np.asarray(inp["lambda_q"][l], f32).reshape(1, 128)
        cst[:, C_LK + l * 128:C_LK + (l + 1) * 128] = np.asarray(inp["lambda_k"][l], f32).reshape(1, 128)
    cst[:, C_EPS] = EPS
    sl = _slopes()
    p = np.arange(128, dtype=np.float64)
    for h in range(4):
        for d in range(19):
            cst[:, C_ALIBI + h * 19 + d] = (sl[12 + h] * (p + 128.0 * (d - 15))).astype(f32)
    cb = np.zeros((128, NCB), f32)
    cb[:, 0:128] = 1.0
    cb[0:64, 128:192] = 1.0
    cb[64:128, 192:256] = 1.0
    k = np.arange(128)[:, None].astype(np.float64)
    q = np.arange(128)[None, :].astype(np.float64)
    for j in range(4):
        s = 2.0 ** (-(j + 1) / 2.0)
        prev = np.where(q <= k, np.exp(-s * (128.0 + q - k)), 0.0)
        cur = np.where(q >= k, np.exp(-s * (q - k)), 0.0)
        cb[:, 256 + (2 * j) * 128:256 + (2 * j + 1) * 128] = prev
        cb[:, 256 + (2 * j + 1) * 128:256 + (2 * j + 2) * 128] = cur
    cb[:, 1280:1408] = (q >= k)
    cb[:, 1408:1536] = np.eye(128, dtype=np.float32)
    cb[:, 1536:1664] = np.where(q >= k, 0.0, -30000.0)
    return cst, cb


_NC_CACHE = {}


def kernel(**inputs):
    x = np.asarray(inputs["x"], np.float32)
    f_wi, f_wo, m_wi, m_fin, m_wo = _prep_weights(inputs)
    cst, cb = _prep_consts(inputs)
    if "nc" not in _NC_CACHE:
        _NC_CACHE["nc"] = build_nc()
    nc = _NC_CACHE["nc"]
    in_maps = []
    for b in range(8):
        xT = np.ascontiguousarray(x[b].T).reshape(8, 128, S)
        in_maps.append({"xT": xT, "f_wi": f_wi, "f_wo": f_wo, "m_wi": m_wi, "m_fin": m_fin,
                        "m_wo": m_wo, "cst": cst, "cb": cb})
    res = run_bass_kernel_spmd(nc, in_maps, core_ids=list(range(8)))
    out = np.empty((8, S, D), np.float32)
    for b in range(8):
        out[b] = np.asarray(res.results[b]["outT"], np.float32).reshape(D, S).T
    return out
```

```python
import math
from contextlib import ExitStack

import numpy as np
import concourse.bass as bass
import concourse.mybir as mybir
from concourse.bass_utils import run_bass_kernel_spmd

F32 = mybir.dt.float32
BF16 = mybir.dt.bfloat16
AF = mybir.ActivationFunctionType
ALU = mybir.AluOpType

D = 1024
S = 2048
L = 2
FF = 2816
NFC = 22
HD = 64
EPS = 1e-6
GROUPS = [list(range(0, 5)), list(range(5, 10)), list(range(10, 16)), list(range(16, 22))]
DILS = (1, 4, 16)


class Reg:
    __slots__ = ("name", "w", "r", "excl")

    def __init__(self, name, excl=False):
        self.name = name
        self.w = None
        self.r = []
        self.excl = excl


class Eng:
    def __init__(self, key):
        self.key = key
        self.count = 0
        self.pending = False
        self.seen = {}
        self.ops = []


class Prog:
    ENGS = ("pe", "act", "dve", "pool", "sp")

    def __init__(self):
        self.e = {k: Eng(k) for k in self.ENGS}
        self.dma_sems = {}

    def _collect(self, eng, reads, writes):
        need = {}

        def add(ev, same_ok):
            if ev is None:
                return
            k, v = ev
            if k == eng.key and not same_ok:
                return
            if need.get(k, 0) < v:
                need[k] = v

        for r in reads:
            add(r.w, True)
            if r.excl:
                for ev in r.r:
                    add(ev, False)
        same = eng.key != "pe"
        for w in writes:
            add(w.w, same)
            for ev in w.r:
                add(ev, same)
        waits = []
        for k, v in need.items():
            if eng.seen.get(k, 0) >= v:
                continue
            eng.seen[k] = v
            waits.append((k, v))
        return waits

    def op(self, engkey, fn, reads=(), writes=(), inc=True):
        eng = self.e[engkey]
        waits = self._collect(eng, reads, writes)
        ev = (engkey, eng.count + 1)
        if inc:
            eng.count += 1
            eng.pending = False
        else:
            eng.pending = True
        for r in reads:
            r.r.append(ev)
            if len(r.r) > 64:
                r.r = _compress(r.r)
        for w in writes:
            w.w = ev
            w.r = []
        eng.ops.append((waits, fn, inc, None))

    def dma(self, engkey, semname, fn, reads=(), writes=()):
        eng = self.e[engkey]
        waits = self._collect(eng, reads, writes)
        self.dma_sems[semname] = self.dma_sems.get(semname, 0) + 16
        ev = ("dma:" + semname, self.dma_sems[semname])
        for r in reads:
            r.r.append(ev)
        for w in writes:
            w.w = ev
            w.r = []
        eng.ops.append((waits, fn, False, semname))

    def wait_all(self, engkey, regs):
        eng = self.e[engkey]
        waits = self._collect(eng, regs, regs)
        eng.ops.append((waits, None, False, None))

    def emit(self, nc, stack):
        sems = {}
        for k in self.ENGS:
            sems[k] = stack.enter_context(nc.semaphore("s_" + k))
        for name in self.dma_sems:
            sems["dma:" + name] = stack.enter_context(nc.semaphore("d_" + name))
        for k in self.ENGS:
            assert not self.e[k].pending, k
        block = stack.enter_context(nc.Block())
        hooks = {"pe": block.tensor, "act": block.scalar, "dve": block.vector,
                 "pool": block.gpsimd, "sp": block.sync}

        def make(k):
            ops = self.e[k].ops

            def body(h):
                for waits, fn, inc, dsem in ops:
                    if fn is None:
                        for (sk, v) in waits:
                            h.wait_ge(sems[sk], v)
                        continue
                    for (sk, v) in waits[:-1]:
                        h.wait_ge(sems[sk], v)
                    ins = fn(h)
                    if waits:
                        ins._wait_ge(sems[waits[-1][0]], waits[-1][1])
                    if inc:
                        ins.then_inc(sems[k], 1)
                    if dsem is not None:
                        ins.then_inc(sems["dma:" + dsem], 16)
            return body

        for k in self.ENGS:
            hooks[k](make(k))


def _compress(evs):
    best = {}
    for k, v in evs:
        if best.get(k, 0) < v:
            best[k] = v
    return list(best.items())


class Arena:
    def __init__(self, tensor, ncols):
        self.t = tensor
        self.n = ncols
        self.live = []

    def region(self, name, off, ncols):
        assert off + ncols <= self.n, (name, off, ncols, self.n)
        reg = Reg(name)
        keep = []
        for (a, b, r) in self.live:
            if a < off + ncols and off < b:
                if r.w is not None:
                    reg.r.append(r.w)
                reg.r.extend(r.r)
            else:
                keep.append((a, b, r))
        reg.r = _compress(reg.r)
        keep.append((off, off + ncols, reg))
        self.live = keep
        return self.t[:, off:off + ncols], reg

    def multi(self, name, off, ncols, count):
        ap, parent = self.region(name, off, ncols)
        regs = []
        for i in range(count):
            r = Reg(f"{name}.{i}")
            r.r = list(parent.r)
            regs.append(r)
            self.live.append((off, off + ncols, r))
        return ap, regs


C_NORM = 0
C_QKDIL = C_NORM + 48
C_QKDIFF = C_QKDIL + 24
C_SUB = C_QKDIFF + 16
C_EPS = C_SUB + 8
C_LAMI = C_EPS + 1
C_OML = C_LAMI + 2
C_ALIBI = C_OML + 2
C_LQ = C_ALIBI + 76
C_LK = C_LQ + 256
NCST = C_LK + 256
C_NLAM = NCST
C_SUBS = C_NLAM + 2
NCST_ALL = C_SUBS + 8

NCB = 11 * 128


def _slopes():
    return [2.0 ** (-8.0 * (i + 1) / 16.0) for i in range(16)]


def build_nc(n_layers=L, stages=("ffn1", "mix", "ffn2"), mix_parts=("dil", "diff", "fin")):
    nc = bass.Bass("TRN2", target_bir_lowering=False)
    dt_in = lambda name, shape: nc.dram_tensor(name, shape, F32, kind="ExternalInput").ap()
    xT_d = dt_in("xT", [8, 128, S])
    f_wi = dt_in("f_wi", [L * 2 * NFC, 128, 2048])
    f_wo = dt_in("f_wo", [L * 2 * NFC, 128, 1024])
    m_wi = dt_in("m_wi", [L * 30, 128, 1024])
    m_fin = dt_in("m_fin", [L * 8, 128, 3072])
    m_wo = dt_in("m_wo", [L, 128, 8192])
    cst_d = dt_in("cst", [128, NCST])
    cb_d = dt_in("cb", [128, NCB])
    out_d = nc.dram_tensor("outT", [8, 128, S], F32, kind="ExternalOutput").ap()

    P = Prog()
    NAR = 16384 + 43008
    with ExitStack() as st:
        sb = lambda n, sh, dt: st.enter_context(nc.sbuf_tensor(n, sh, dt))
        xT = sb("xT_sb", [128, 8, S], F32)
        ar_t = sb("arena", [128, NAR], BF16)
        scr = sb("scr", [128, 8, 512], F32)
        cst = sb("cst_sb", [128, NCST_ALL], F32)
        cb = sb("cb_sb", [128, NCB], BF16)
        lamtmp = sb("lamtmp", [128, 8], F32)
        lamprod = sb("lamprod", [128, 256], F32)
        ps_all = st.enter_context(nc.psum_tensor("ps_all", [128, 8, 512], F32))
        banks = [ps_all[:, i, :] for i in range(8)]
        BR = [Reg(f"bank{i}", excl=True) for i in range(8)]
        AR = Arena(ar_t, NAR)

        XR = [[Reg(f"x{n}_{t}") for t in range(4)] for n in range(8)]
        SCR = [Reg(f"scr{i}") for i in range(8)]
        CST = Reg("cst")
        CB = Reg("cb")
        LAM = Reg("lam")
        OUT = Reg("out")

        ones_bf = cb[:, 0:128]
        bd_bf = cb[:, 128:256]
        maskv = cb[:, 256:256 + 1024].rearrange("p (m q) -> p m q", m=8)
        tri_bf = cb[:, 1280:1408]

        def cs(col):
            return cst[:, col:col + 1]

        def mm(out, lhsT, rhs, start, stop, reads, writes, inc, skip=False):
            if skip:
                P.op("pe", lambda e: e.matmul(out, lhsT=lhsT, rhs=rhs, start=start, stop=stop,
                                              skip_group_check=True), reads, writes, inc)
            else:
                P.op("pe", lambda e: e.matmul(out, lhsT=lhsT, rhs=rhs, start=start, stop=stop),
                     reads, writes, inc)

        def act(out, in_, func, reads, writes, bias=None, scale=None):
            kw = {}
            if bias is not None:
                kw["bias"] = bias
            if scale is not None:
                kw["scale"] = scale
            P.op("act", lambda e: e.activation(out=out, in_=in_, func=func, **kw), reads, writes)

        def tt(eng, out, in0, in1, op, reads, writes):
            P.op(eng, lambda e: e.tensor_tensor(out=out, in0=in0, in1=in1, op=op), reads, writes)

        def stt(eng, out, in0, scalar, in1, op0, op1, reads, writes):
            P.op(eng, lambda e: e.scalar_tensor_tensor(out=out, in0=in0, scalar=scalar, in1=in1,
                                                       op0=op0, op1=op1), reads, writes)

        def recip(out, in_, reads, writes):
            P.op("dve", lambda e: e.reciprocal(out=out, in_=in_), reads, writes)

        def rsqrt_act(dst, src, scale, reads, dreg):
            act(dst, src, AF.Ln, list(reads) + [CST], [dreg], bias=cs(C_EPS), scale=scale)
            act(dst, dst, AF.Exp, [dreg], [dreg], scale=-0.5)

        def recip_act(dst, src, reads, dreg):
            act(dst, src, AF.Ln, list(reads), [dreg])
            act(dst, dst, AF.Exp, [dreg], [dreg], scale=-1.0)

        def pipeline(units, depth=1):
            n = len(units)
            for i in range(min(depth, n)):
                units[i][0]()
            for i in range(n):
                if i + depth < n:
                    units[i + depth][0]()
                units[i][1]()

        def wdma(sem, out, in_, reads, writes):
            P.dma("pool", sem, lambda e: e.dma_start(out=out, in_=in_), reads, writes)

        P.dma("sp", "cst", lambda e: e.dma_start(out=cst[:, 0:NCST], in_=cst_d), writes=[CST])
        wdma("cb", cb[:], cb_d, [], [CB])
        xT_dv = xT_d.rearrange("n p s -> p n s")
        for t in range(4):
            for hf in range(2):
                P.dma("act" if (t == 0 and hf == 1) else "sp", f"x{t * 2 + hf}",
                      (lambda t, hf: lambda e: e.dma_start(out=xT[:, 4 * hf:4 * hf + 4, 512 * t:512 * t + 512],
                                                           in_=xT_dv[:, 4 * hf:4 * hf + 4, 512 * t:512 * t + 512]))(t, hf),
                      writes=[XR[n][t] for n in range(4 * hf, 4 * hf + 4)])

        hT_ap, _ = AR.region("hT", 0, 16384)
        hT = hT_ap.rearrange("p (k s) -> p k s", k=8)
        HR = [[Reg(f"h{k}_{t}") for t in range(4)] for k in range(8)]
        BASE = 16384

        def tsl(t):
            return slice(512 * t, 512 * t + 512)

        nsq_t = sb("nsq", [128, 2, 512], BF16)
        NSQR = [Reg("nsq0"), Reg("nsq1")]

        def norm_stats(gcol, t):
            sb_i = 6 + (t % 2)
            for k in range(8):
                i = k % 2
                act(nsq_t[:, i, :], xT[:, k, tsl(t)], AF.Square, [XR[k][t]], [NSQR[i]])
                mm(banks[sb_i][:], ones_bf, nsq_t[:, i, :], k == 0, k == 7, [CB, NSQR[i]], [BR[sb_i]], True)
            si = t % 2
            rsqrt_act(scr[:, si, :], banks[sb_i][:], 1.0 / D, [BR[sb_i]], SCR[si])

        def norm_write(gcol, t, k):
            si = t % 2
            stt("dve", hT[:, k, tsl(t)], xT[:, k, tsl(t)], cs(gcol + k), scr[:, si, :],
                ALU.mult, ALU.mult, [XR[k][t], SCR[si], CST], [HR[k][t]])

        def norm_tile(gcol, t):
            norm_stats(gcol, t)
            for k in range(8):
                norm_write(gcol, t, k)

        def rmsnorm(gcol):
            for t in range(4):
                norm_tile(gcol, t)

        def ffn(l, f, do_norm=True, next_g=None):
            if do_norm:
                rmsnorm(C_NORM + (l * 3 + (0 if f == 0 else 2)) * 8)
            tbase = (l * 2 + f) * NFC
            act_ap, act_rs = AR.multi("act", BASE, 12288, 24)
            actv = act_ap.rearrange("p (c s) -> p c s", c=6)
            ACTR = [[act_rs[c * 4 + t] for t in range(4)] for c in range(6)]
            win, winr = [], []
            for i in range(3):
                a, r = AR.region(f"win{i}", BASE + 28672 + 2048 * i, 2048)
                win.append(a.rearrange("p (k c) -> p k c", k=8))
                winr.append(r)
            wout, woutr = [], []
            for i in range(12):
                a, r = AR.region(f"wout{i}", BASE + 12288 + 1024 * i, 1024)
                wout.append(a)
                woutr.append(r)

            x0 = [XR[n][0] for n in range(8)] if (l == 0 and f == 0) else []

            def load_in(c):
                wdma(f"win{c % 3}", win[c % 3], f_wi[tbase + c].rearrange("p (k c) -> p k c", k=8),
                     x0 if c in (1, 2) else [], [winr[c % 3]])

            def load_out(c):
                wdma(f"wout{c % 12}", wout[c % 12], f_wo[tbase + c], x0 if c < 3 else [], [woutr[c % 12]])

            def load(c):
                load_in(c)
                load_out(c)

            for c in range(3):
                load_in(c)
            for c in range(3):
                load_out(c)
            pair = 0
            ybank = 0
            for grp in GROUPS:
                for ci, c in enumerate(grp):
                    w = win[c % 3]
                    wr = winr[c % 3]
                    for t in range(4):
                        bg, bu = (0, 1) if pair % 2 == 0 else (2, 3)
                        pair += 1
                        for k in range(8):
                            mm(banks[bg][:], w[:, k, 0:128], hT[:, k, tsl(t)], k == 0, k == 7,
                               [wr, HR[k][t]], [BR[bg]], k == 7)
                        for k in range(8):
                            mm(banks[bu][:], w[:, k, 128:256], hT[:, k, tsl(t)], k == 0, k == 7,
                               [wr, HR[k][t]], [BR[bu]], k == 7)
                        si = 2 + (pair % 2)
                        act(scr[:, si, :], banks[bg][:], AF.Silu, [BR[bg]], [SCR[si]])
                        tt("dve", actv[:, ci, tsl(t)], scr[:, si, :], banks[bu][:], ALU.mult,
                           [SCR[si], BR[bu]], [ACTR[ci][t]])
                    if c + 3 < NFC:
                        load(c + 3)
                for t in range(4):
                    for n in range(8):
                        yb = 4 + (ybank % 2)
                        ybank += 1
                        for ci, c in enumerate(grp):
                            mm(banks[yb][:], wout[c % 12][:, n * 128:(n + 1) * 128], actv[:, ci, tsl(t)],
                               ci == 0, ci == len(grp) - 1, [woutr[c % 12], ACTR[ci][t]], [BR[yb]],
                               ci == len(grp) - 1)
                        stt("dve", xT[:, n, tsl(t)], banks[yb][:], 0.5, xT[:, n, tsl(t)], ALU.mult, ALU.add,
                            [BR[yb], XR[n][t]], [XR[n][t]])
                        if next_g is not None and grp is GROUPS[-1] and t > 0:
                            norm_write(next_g, t - 1, n)
                    if next_g is not None and grp is GROUPS[-1]:
                        norm_stats(next_g, t)
                        if t == 3:
                            for k in range(8):
                                norm_write(next_g, 3, k)

        class WRing:
            def __init__(self, name, off, n, cols):
                self.name = name
                self.n = n
                self.ap, self.reg = [], []
                for i in range(n):
                    a, r = AR.region(f"{name}{i}", off + cols * i, cols)
                    self.ap.append(a)
                    self.reg.append(r)
                self.i = 0

            def load(self, src):
                i = self.i % self.n
                self.i += 1
                wdma(f"{self.name}{i}", self.ap[i], src, [], [self.reg[i]])
                return self.ap[i], self.reg[i]

            def set_queue(self, srcs):
                self.q = list(srcs)
                self.issued = 0
                self.taken = 0
                self.slots = []

            def prime(self):
                while self.issued < len(self.q) and self.issued < self.taken + self.n:
                    self.slots.append(self.load(self.q[self.issued]))
                    self.issued += 1

            def nxt(self):
                self.prime()
                r = self.slots[self.taken]
                self.taken += 1
                return r

        def qk_project(w_ap, w_reg, gcol, dst, dst_reg, dil, sq_ap, sqr, rawbanks=(0, 1, 2, 3)):
            wv = w_ap.rearrange("p (k c) -> p k c", k=8)
            units = []
            for t in range(4):
                def head(t=t):
                    rb = rawbanks[t % len(rawbanks)]
                    for k in range(8):
                        mm(banks[rb][:], wv[:, k, :], hT[:, k, tsl(t)], k == 0, k == 7,
                           [w_reg, HR[k][t]], [BR[rb]], k == 7)
                    i = t % 2
                    act(sq_ap[i], banks[rb][:], AF.Square, [BR[rb]], [sqr[i]])

                def tail(t=t):
                    rb = rawbanks[t % len(rawbanks)]
                    i = t % 2
                    sbk = 6 + (t % 2)
                    mm(banks[sbk][:], bd_bf, sq_ap[i], True, True, [CB, sqr[i]], [BR[sbk]], True)
                    si = 6 + (t % 2)
                    rsqrt_act(scr[:, si, :], banks[sbk][:], 1.0 / HD, [BR[sbk]], SCR[si])
                    if dil == 1:
                        o = dst[:, tsl(t)]
                        i0 = banks[rb][:]
                        i1 = scr[:, si, :]
                    else:
                        n_per = 512 // dil
                        o = dst.rearrange("p (c i) -> p c i", c=dil)[:, :, n_per * t:n_per * (t + 1)]
                        i0 = banks[rb][:].rearrange("p (i c) -> p c i", c=dil)
                        i1 = scr[:, si, :].rearrange("p (i c) -> p c i", c=dil)
                    stt("dve", o, i0, cs(gcol), i1, ALU.mult, ALU.mult, [BR[rb], SCR[si], CST], [dst_reg])
                units.append((head, tail))
            pipeline(units)

        def dilated(l, odil, ODR, mw):
            off = BASE + 4096
            qk_ap, qk_reg = {}, {}
            for g in range(3):
                for which in range(2):
                    a, r = AR.region(f"dqk{g}{which}", off, 2048)
                    qk_ap[(g, which)] = a
                    qk_reg[(g, which)] = r
                    off += 2048
            v_ap, v_reg = {}, {}
            for g in range(3):
                a, r = AR.region(f"dv{g}", off, 4096)
                v_ap[g] = a.rearrange("p (b h e) -> p b h e", b=16, h=2)
                v_reg[g] = r
                off += 4096
            ring = mw
            e_ap, e_reg = [], []
            for i in range(6):
                a, r = AR.region(f"dE{i}", off, 512)
                e_ap.append(a)
                e_reg.append(r)
                off += 512
            sq_ap, sqr = [], []
            for i in range(2):
                a, r = AR.region(f"dsq{i}", off, 512)
                sq_ap.append(a)
                sqr.append(r)
                off += 512
            assert off <= NAR - 4096
            ecnt = [0]

            for jp in range(2):
                for g in range(3):
                    if jp != 0:
                        break
                    P.op("pool", (lambda g: lambda e: e.memset(v_ap[g][:, :, 0, 64:128], 1.0))(g), [], [v_reg[g]])
                    P.op("pool", (lambda g: lambda e: e.memset(v_ap[g][:, :, 1, 0:64], 1.0))(g), [], [v_reg[g]])
                for g in range(3):
                    dil = DILS[g]
                    for which in range(2):
                        w_ap, w_reg = ring.nxt()
                        gcol = C_QKDIL + ((l * 2 + which) * 3 + g) * 2 + jp
                        qk_project(w_ap, w_reg, gcol, qk_ap[(g, which)], qk_reg[(g, which)], dil, sq_ap, sqr)
                    w_ap, w_reg = ring.nxt()
                    wv = w_ap.rearrange("p (k c) -> p k c", k=8)
                    for b4 in range(4):
                        vb = 4 + (b4 % 2)
                        for bb in range(4):
                            blk = b4 * 4 + bb
                            nblk = 16 // dil
                            c, n = blk // nblk, blk % nblk
                            start_tok = 128 * n * dil + c
                            for k in range(8):
                                lhsT = hT[:, k, start_tok:start_tok + 127 * dil + 1:dil] if dil > 1 else \
                                    hT[:, k, start_tok:start_tok + 128]
                                mm(banks[vb][:, bb * 128:(bb + 1) * 128], lhsT, wv[:, k, :], k == 0, k == 7,
                                   [w_reg] + [HR[k][tt_] for tt_ in range(4)], [BR[vb]],
                                   (k == 7 and bb == 3))
                        src = banks[vb][:].rearrange("p (b h e) -> p b h e", b=4, h=2)
                        P.op("act", (lambda g, b4, src: lambda e: e.activation(
                            out=v_ap[g][:, 4 * b4:4 * b4 + 4, 0, 0:64], in_=src[:, :, 0, :], func=AF.Copy))(g, b4, src),
                            [BR[vb]], [v_reg[g]])
                        P.op("act", (lambda g, b4, src: lambda e: e.activation(
                            out=v_ap[g][:, 4 * b4:4 * b4 + 4, 1, 64:128], in_=src[:, :, 1, :], func=AF.Copy))(g, b4, src),
                            [BR[vb]], [v_reg[g]])
                units = []
                for hl in range(2):
                    j = 2 * jp + hl
                    ps = slice(64 * hl, 64 * hl + 64)
                    num = slice(0, 64) if hl == 0 else slice(64, 128)
                    den = slice(64, 128) if hl == 0 else slice(0, 64)
                    for n in range(4):
                        ob = 4 + (n % 2)
                        jobs = []
                        for which, mid in ((0, 0), (1, 1)):
                            items = []
                            for qb in range(4):
                                b = 4 * n + qb
                                kb = b - 1 if which == 0 else b
                                if kb < 0:
                                    continue
                                items.append((qk_ap[(0, 1)][ps, 128 * kb:128 * kb + 128],
                                              qk_ap[(0, 0)][ps, 128 * b:128 * b + 128],
                                              kb, banks[ob][:, 128 * qb:128 * qb + 128], 128, 128 * qb))
                            jobs.append((0, mid, items))
                        for which, mid in ((0, 0), (1, 1)):
                            items = []
                            for c in range(4):
                                kn = n - 1 if which == 0 else n
                                if kn < 0:
                                    continue
                                items.append((qk_ap[(1, 1)][ps, 512 * c + 128 * kn:512 * c + 128 * kn + 128],
                                              qk_ap[(1, 0)][ps, 512 * c + 128 * n:512 * c + 128 * n + 128],
                                              4 * c + kn, banks[ob][:, c:512:4], 128, 128 * c))
                            jobs.append((1, mid, items))
                        items = []
                        for c in range(16):
                            items.append((qk_ap[(2, 1)][ps, 128 * c:128 * c + 128],
                                          qk_ap[(2, 0)][ps, 128 * c + 32 * n:128 * c + 32 * n + 32],
                                          c, banks[ob][:, c:512:16], 32, 32 * c))
                        jobs.append((2, 2, items))
                        jobs = [jb for jb in jobs if jb[2]]
                        for ji, (g, mid, items) in enumerate(jobs):
                            sbk = ecnt[0] % 4
                            ei = ecnt[0] % 6
                            ecnt[0] += 1
                            first = ji == 0
                            last = ji == len(jobs) - 1

                            def head(g=g, items=items, sbk=sbk, first=first, ob=ob):
                                if first:
                                    P.op("dve", lambda e: e.memset(banks[ob][:], 0.0), [], [BR[ob]])
                                for ii, (k_ap, q_ap, vblk, dest, ncols, col0) in enumerate(items):
                                    mm(banks[sbk][:, col0:col0 + ncols], k_ap, q_ap, True, True,
                                       [qk_reg[(g, 1)], qk_reg[(g, 0)]], [BR[sbk]], ii == len(items) - 1, skip=True)

                            def tail(g=g, mid=mid, items=items, sbk=sbk, ei=ei, first=first, last=last,
                                     ob=ob, n=n, j=j, hl=hl, num=num, den=den):
                                lo = items[0][5]
                                hi = items[-1][5] + items[-1][4]
                                act(e_ap[ei][:, lo:hi], banks[sbk][:, lo:hi], AF.Exp, [BR[sbk]], [e_reg[ei]],
                                    scale=0.125)
                                if mid == 2:
                                    mk = maskv[:, 2 * j + 1:2 * j + 2, 32 * n:32 * n + 32].broadcast_to([128, 16, 32])
                                    ev = e_ap[ei][:, :].rearrange("p (c q) -> p c q", c=16)
                                else:
                                    nb = (hi - lo) // 128
                                    mk = maskv[:, 2 * j + mid:2 * j + mid + 1, :].broadcast_to([128, nb, 128])
                                    ev = e_ap[ei][:, lo:hi].rearrange("p (c q) -> p c q", c=nb)
                                tt("dve", ev, ev, mk, ALU.mult, [e_reg[ei], CB], [e_reg[ei]])
                                for ii, (k_ap, q_ap, vblk, dest, ncols, col0) in enumerate(items):
                                    mm(dest, v_ap[g][:, vblk, hl, :], e_ap[ei][:, col0:col0 + ncols], False, False,
                                       [v_reg[g], e_reg[ei]], [BR[ob]], ii == len(items) - 1, skip=True)
                                if deferred:
                                    deferred.pop(0)()
                                if last:
                                    def fin_piece(ob=ob, n=n, num=num, den=den):
                                        si = 4 + (n % 2)
                                        recip_act(scr[num, si, :], banks[ob][den, :], [BR[ob]], SCR[si])
                                        tt("dve", odil[num, jp, tsl(n)], banks[ob][num, :], scr[num, si, :], ALU.mult,
                                           [BR[ob], SCR[si]], [ODR[jp]])
                                    deferred.append(fin_piece)
                            units.append((head, tail))
                deferred = []
                pipeline(units, depth=2)
                while deferred:
                    deferred.pop(0)()

        def diffattn(l, odiff, OFR, mw, before_last_attn=None):
            off = BASE + 4096 + 8192
            q_ap, q_reg, k_ap, k_reg, v_ap, v_reg = [], [], [], [], [], []
            for i in range(2):
                a, r = AR.region(f"fq{i}", off, 2048)
                q_ap.append(a)
                q_reg.append(r)
                off += 2048
                a, r = AR.region(f"fk{i}", off, 2048)
                k_ap.append(a)
                k_reg.append(r)
                off += 2048
                a, r = AR.region(f"fv{i}", off, 2048)
                v_ap.append(a.rearrange("p (b e) -> p b e", b=16))
                v_reg.append(r)
                off += 2048
            ring = mw
            ep_ap, ep_reg = [], []
            for i in range(3):
                a, r = AR.region(f"fEp{i}", off, 1024)
                ep_ap.append(a.rearrange("p (m q) -> p m q", m=2))
                ep_reg.append(r)
                off += 1024
            sq_ap, sqr = [], []
            for i in range(2):
                a, r = AR.region(f"fsq{i}", off, 512)
                sq_ap.append(a)
                sqr.append(r)
                off += 512
            assert off <= NAR
            P.op("dve", lambda e: e.tensor_tensor(out=lamprod[:, 0:128],
                                                  in0=cst[:, C_LQ + l * 128:C_LQ + l * 128 + 128],
                                                  in1=cst[:, C_LK + l * 128:C_LK + l * 128 + 128], op=ALU.mult),
                 [CST], [LAM])
            P.op("dve", lambda e: e.tensor_reduce(out=lamtmp[:, 0:2],
                                                  in_=lamprod[:, 0:128].rearrange("p (a b) -> p a b", a=2),
                                                  axis=mybir.AxisListType.X, op=ALU.add), [LAM], [LAM])
            act(lamtmp[:, 2:4], lamtmp[:, 0:2], AF.Exp, [LAM], [LAM])
            tt("dve", lamtmp[:, 4:5], lamtmp[:, 3:4], lamtmp[:, 2:3], ALU.subtract, [LAM], [LAM])
            tt("dve", cst[:, C_NLAM + l:C_NLAM + l + 1], lamtmp[:, 4:5], cs(C_LAMI + l), ALU.subtract,
               [LAM, CST], [CST])
            P.op("dve", lambda e: e.tensor_scalar(out=cst[:, C_SUBS + 4 * l:C_SUBS + 4 * l + 4],
                                                  in0=cst[:, C_SUB + 4 * l:C_SUB + 4 * l + 4],
                                                  scalar1=cs(C_OML + l), scalar2=None, op0=ALU.mult),
                 [CST], [CST])

            cnt = [0]
            ecnt = [0]
            deferred = []
            pre = []
            for h in range(4):
                i = h % 2
                w_ap, w_reg = ring.nxt()
                qk_project(w_ap, w_reg, C_QKDIFF + (l * 2 + 0) * 4 + h, q_ap[i], q_reg[i], 1, sq_ap, sqr)
                for _ in range(2):
                    if deferred:
                        deferred.pop(0)(7)
                w_ap, w_reg = ring.nxt()
                qk_project(w_ap, w_reg, C_QKDIFF + (l * 2 + 1) * 4 + h, k_ap[i], k_reg[i], 1, sq_ap, sqr)
                while deferred:
                    deferred.pop(0)(7)
                w_ap, w_reg = ring.nxt()
                wv = w_ap.rearrange("p (k c) -> p k c", k=8)
                for b4 in range(4):
                    vb = 2 + (b4 % 2)
                    for bb in range(4):
                        blk = b4 * 4 + bb
                        for k in range(8):
                            mm(banks[vb][:, bb * 128:(bb + 1) * 128], hT[:, k, 128 * blk:128 * blk + 128],
                               wv[:, k, :], k == 0, k == 7, [w_reg, HR[k][blk // 4]], [BR[vb]],
                               (k == 7 and bb == 3))
                    src = banks[vb][:].rearrange("p (b e) -> p b e", b=4)
                    P.op("act", (lambda i, b4, src: lambda e: e.activation(
                        out=v_ap[i][:, 4 * b4:4 * b4 + 4, :], in_=src, func=AF.Copy))(i, b4, src),
                        [BR[vb]], [v_reg[i]])
                if h == 3 and before_last_attn is not None:
                    before_last_attn()
                units = []
                SBP = (0, 6)
                tri2 = tri_bf.unsqueeze(1).broadcast_to([128, 2, 128])
                for t in range(4):
                    nkb = 4 * t + 4
                    for kb in range(nkb):
                        dk = kb - 4 * t
                        qlo = 0 if dk < 0 else 128 * dk
                        fin = kb == nkb - 1
                        slot = {}

                        def head(kb=kb, qlo=qlo, slot=slot, t=t):
                            sb0 = slot["sb0"] = SBP[cnt[0] % 2]
                            cnt[0] += 1
                            for m in range(2):
                                ps = slice(64 * m, 64 * m + 64)
                                mm(banks[sb0 + m][:, qlo:512], k_ap[i][ps, 128 * kb:128 * kb + 128],
                                   q_ap[i][ps, 512 * t + qlo:512 * t + 512], True, True,
                                   [k_reg[i], q_reg[i]], [BR[sb0 + m]], True)

                        def tail(kb=kb, dk=dk, qlo=qlo, slot=slot, t=t, nkb=nkb, fin=fin):
                            sb0 = slot["sb0"]
                            ei = ecnt[0] % 3
                            ecnt[0] += 1
                            ep = ep_ap[ei]
                            act(ep[:, :, qlo:512], ps_all[:, sb0:sb0 + 2, qlo:512], AF.Exp,
                                [BR[sb0], BR[sb0 + 1], CST], [ep_reg[ei]],
                                bias=cs(C_ALIBI + h * 19 + (dk + 15)), scale=0.125)
                            had_pre = bool(pre)
                            while pre:
                                pre.pop(0)()
                            if dk >= 0:
                                tt("dve", ep[:, :, qlo:qlo + 128], ep[:, :, qlo:qlo + 128], tri2, ALU.mult,
                                   [ep_reg[ei], CB], [ep_reg[ei]])
                            order = ((2, 0), (2, 1), (4, 0), (4, 1)) if had_pre else ((4, 0), (2, 0), (4, 1), (2, 1))
                            for (bb, m) in order:
                                if bb == 4:
                                    mm(banks[4 + m][:, qlo:512], ones_bf, ep[:, m, qlo:512], kb == 0, kb == nkb - 1,
                                       [CB, ep_reg[ei]], [BR[4 + m]], True, skip=True)
                                else:
                                    mm(banks[2 + m][:, qlo:512], v_ap[i][:, kb, :], ep[:, m, qlo:512], kb == 0,
                                       kb == nkb - 1, [v_reg[i], ep_reg[ei]], [BR[2 + m]], True, skip=True)
                            if deferred:
                                deferred.pop(0)(sb0)
                            if not fin:
                                return
                            for m in range(2):
                                P.op("dve", (lambda m: lambda e: e.tensor_copy(out=scr[:, m, :], in_=banks[2 + m][:]))(m),
                                     [BR[2 + m]], [SCR[m]])
                                pre.append((lambda m: lambda: act(scr[:, 2 + m, :], banks[4 + m][:], AF.Ln, [BR[4 + m]],
                                                                  [SCR[2 + m]]))(m))

                            def piece_b1(fb):
                                act(scr[:, 2, :], scr[:, 2, :], AF.Exp, [SCR[2]], [SCR[2]], scale=-1.0)
                                tt("dve", scr[:, 0, :], scr[:, 0, :], scr[:, 2, :], ALU.mult, [SCR[0], SCR[2]], [SCR[0]])

                            def piece_b2(fb):
                                act(scr[:, 3, :], scr[:, 3, :], AF.Exp, [SCR[3]], [SCR[3]], scale=-1.0)
                                tt("dve", scr[:, 1, :], scr[:, 1, :], scr[:, 3, :], ALU.mult, [SCR[1], SCR[3]], [SCR[1]])
                                stt("dve", scr[:, 0, :], scr[:, 1, :], cs(C_NLAM + l), scr[:, 0, :], ALU.mult, ALU.add,
                                    [SCR[0], SCR[1], CST], [SCR[0]])

                            def piece_c1(fb, t=t):
                                si = t % 2
                                tt("dve", sq_ap[si], scr[:, 0, :], scr[:, 0, :], ALU.mult, [SCR[0]], [sqr[si]])
                                mm(banks[fb][:], ones_bf, sq_ap[si], True, True, [CB, sqr[si]], [BR[fb]], True)
                                act(scr[:, 2, :], banks[fb][:], AF.Ln, [BR[fb], CST], [SCR[2]], bias=cs(C_EPS), scale=1.0 / 128)

                            def piece_c2(fb, t=t, h=h):
                                act(scr[:, 2, :], scr[:, 2, :], AF.Exp, [SCR[2]], [SCR[2]], scale=-0.5)
                                stt("dve", odiff[:, h, tsl(t)], scr[:, 0, :], cs(C_SUBS + 4 * l + h), scr[:, 2, :],
                                    ALU.mult, ALU.mult, [SCR[0], SCR[2], CST], [OFR[h]])
                            deferred.extend([piece_b1, piece_b2, piece_c1, piece_c2])
                        units.append((head, tail))
                pipeline(units, depth=1)
                while pre:
                    pre.pop(0)()
                if h == 3:
                    while deferred:
                        deferred.pop(0)(0)

        def final(l, odil, ODR, odiff, OFR, ring, wo_ap, wo_regs, next_g=None):
            off = BASE + 4096 + 8192
            y_ap, y_rs = AR.multi("yT", off, 16384, 32)
            yT = y_ap.rearrange("p (m s) -> p m s", m=8)
            YR = [[y_rs[m * 4 + t] for t in range(4)] for m in range(8)]
            wov = wo_ap.rearrange("p (m c) -> p m c", m=8)
            for m in range(8):
                w_ap, w_reg = ring.nxt()
                ga = w_ap[:, 0:1024].rearrange("p (k c) -> p k c", k=8)
                gb = w_ap[:, 1024:2048].rearrange("p (k c) -> p k c", k=8)
                wa = w_ap[:, 2048:2304].rearrange("p (j c) -> p j c", j=2)
                wb = w_ap[:, 2560:3072].rearrange("p (h c) -> p h c", h=4)
                for t in range(4):
                    b0 = 0 if t % 2 == 0 else 4
                    for k in range(8):
                        mm(banks[b0][:], ga[:, k, :], hT[:, k, tsl(t)], k == 0, k == 7, [w_reg, HR[k][t]],
                           [BR[b0]], k == 7)
                    for k in range(8):
                        mm(banks[b0 + 1][:], gb[:, k, :], hT[:, k, tsl(t)], k == 0, k == 7, [w_reg, HR[k][t]],
                           [BR[b0 + 1]], k == 7)
                    for jp in range(2):
                        mm(banks[b0 + 2][:], wa[:, jp, :], odil[:, jp, tsl(t)], jp == 0, jp == 1,
                           [w_reg, ODR[jp]], [BR[b0 + 2]], jp == 1)
                    for h in range(4):
                        mm(banks[b0 + 3][:], wb[:, h, :], odiff[:, h, tsl(t)], h == 0, h == 3,
                           [w_reg, OFR[h]], [BR[b0 + 3]], h == 3)
                    s0 = 0 if t % 2 == 0 else 2
                    act(scr[:, s0, :], banks[b0][:], AF.Sigmoid, [BR[b0]], [SCR[s0]])
                    act(scr[:, s0 + 1, :], banks[b0 + 1][:], AF.Sigmoid, [BR[b0 + 1]], [SCR[s0 + 1]])
                    tt("dve", scr[:, s0, :], scr[:, s0, :], banks[b0 + 2][:], ALU.mult, [SCR[s0], BR[b0 + 2]], [SCR[s0]])
                    tt("dve", scr[:, s0 + 1, :], scr[:, s0 + 1, :], banks[b0 + 3][:], ALU.mult,
                       [SCR[s0 + 1], BR[b0 + 3]], [SCR[s0 + 1]])
                    tt("dve", yT[:, m, tsl(t)], scr[:, s0, :], scr[:, s0 + 1, :], ALU.add,
                       [SCR[s0], SCR[s0 + 1]], [YR[m][t]])
            ob = 0
            for t in range(4):
                for n in range(8):
                    b = ob % 8
                    ob += 1
                    for m in range(8):
                        mm(banks[b][:], wov[:, m, n * 128:(n + 1) * 128], yT[:, m, tsl(t)], m == 0, m == 7,
                           [wo_regs[m // 4], YR[m][t]], [BR[b]], m == 7)
                    tt("dve", xT[:, n, tsl(t)], banks[b][:], xT[:, n, tsl(t)], ALU.add, [BR[b], XR[n][t]], [XR[n][t]])
                    if next_g is not None and t > 0:
                        norm_write(next_g, t - 1, n)
                if next_g is not None:
                    norm_stats(next_g, t)
                    if t == 3:
                        for k in range(8):
                            norm_write(next_g, 3, k)

        def mixer(l, do_norm=True, next_g=None):
            if do_norm:
                rmsnorm(C_NORM + (l * 3 + 1) * 8)
            od_ap, ODR = AR.multi("odil", BASE, 4096, 2)
            odil = od_ap.rearrange("p (j s) -> p j s", j=2)
            of_ap, OFR = AR.multi("odiff", BASE + 4096, 8192, 4)
            odiff = of_ap.rearrange("p (h s) -> p h s", h=4)
            mw = WRing("mw", NAR - 4096, 4, 1024)
            tiles = []
            if "dil" in mix_parts:
                for jp in range(2):
                    for g in range(3):
                        tiles += [g * 2 + jp, 6 + g * 2 + jp, 12 + g * 2 + jp]
            if "diff" in mix_parts:
                for h in range(4):
                    tiles += [18 + h, 22 + h, 26 + h]
            mw.set_queue([m_wi[l * 30 + t_] for t_ in tiles])
            mw.prime()
            if "dil" in mix_parts:
                dilated(l, odil, ODR, mw)
            fin = {}

            def prefetch_final():
                fin["ring"] = WRing("nw", BASE + 4096 + 8192 + 16384, 2, 3072)
                fin["ring"].set_queue([m_fin[l * 8 + m] for m in range(8)])
                fin["ring"].prime()
                off = BASE + 4096 + 8192 + 16384 + 6144
                a0, r0 = AR.region("wo0", off, 4096)
                a1, r1 = AR.region("wo1", off + 4096, 4096)
                fin["wo_ap"] = ar_t[:, off:off + 8192]
                fin["wo_regs"] = [r0, r1]
                wdma("wo_a", a0, m_wo[l][:, 0:4096], [], [r0])
                wdma("wo_b", a1, m_wo[l][:, 4096:8192], [], [r1])

            if "diff" in mix_parts:
                diffattn(l, odiff, OFR, mw, prefetch_final if "fin" in mix_parts else None)
            if "fin" in mix_parts:
                if not fin:
                    prefetch_final()
                final(l, odil, ODR, odiff, OFR, fin["ring"], fin["wo_ap"], fin["wo_regs"], next_g)

        plan = []
        for l in range(n_layers):
            for st_name in ("ffn1", "mix", "ffn2"):
                if st_name in stages:
                    plan.append((st_name, l))
        fuse_norm = "fin" in mix_parts

        def gcol_of(st_name, l):
            return C_NORM + (l * 3 + {"ffn1": 0, "mix": 1, "ffn2": 2}[st_name]) * 8

        for i, (st_name, l) in enumerate(plan):
            do_norm = (i == 0) or not fuse_norm
            after = None
            if fuse_norm and i + 1 < len(plan):
                after = gcol_of(*plan[i + 1])
            if st_name == "ffn1":
                ffn(l, 0, do_norm, after)
            elif st_name == "mix":
                mixer(l, do_norm, after)
            else:
                ffn(l, 1, do_norm, after)

        out_dv = out_d.rearrange("n p s -> p n s")
        for t in range(4):
            for hf in range(2):
                P.dma("sp", "out",
                      (lambda t, hf: lambda e: e.dma_start(out=out_dv[:, 4 * hf:4 * hf + 4, 512 * t:512 * t + 512],
                                                           in_=xT[:, 4 * hf:4 * hf + 4, 512 * t:512 * t + 512]))(t, hf),
                      reads=[XR[n][t] for n in range(4 * hf, 4 * hf + 4)], writes=[OUT])
        P.wait_all("sp", [OUT])
        P.emit(nc, st)
    return nc


def _prep_weights(inp):
    f32 = np.float32
    f_wi = np.empty((L * 2 * NFC, 128, 2048), f32)
    f_wo = np.empty((L * 2 * NFC, 128, 1024), f32)
    for l in range(L):
        for f, (wi, wo) in enumerate(((inp["ffn1_w_in"], inp["ffn1_w_out"]), (inp["ffn2_w_in"], inp["ffn2_w_out"]))):
            w = np.asarray(wi[l], f32)
            g = w[:, :FF].reshape(8, 128, NFC, 128)
            u = w[:, FF:].reshape(8, 128, NFC, 128)
            t = np.stack([g, u], axis=3)
            t = t.transpose(2, 1, 0, 3, 4).reshape(NFC, 128, 2048)
            base = (l * 2 + f) * NFC
            f_wi[base:base + NFC] = t
            f_wo[base:base + NFC] = np.asarray(wo[l], f32).reshape(NFC, 128, 1024)
    m_wi = np.empty((L * 30, 128, 1024), f32)
    m_fin = np.zeros((L * 8, 128, 3072), f32)
    m_wo = np.empty((L, 128, 8192), f32)
    for l in range(L):
        w = np.asarray(inp["w_in"][l], f32)
        tiles = w.reshape(8, 128, 46, 128).transpose(2, 1, 0, 3).reshape(46, 128, 1024)
        m_wi[l * 30:(l + 1) * 30] = tiles[:30]
        wa = np.asarray(inp["w_branch_dil"][l], f32)
        wb = np.asarray(inp["w_branch_diff"][l], f32)
        for m in range(8):
            m_fin[l * 8 + m, :, 0:1024] = tiles[30 + m]
            m_fin[l * 8 + m, :, 1024:2048] = tiles[38 + m]
            a = wa[:, m * 128:(m + 1) * 128].reshape(2, 128, 128).transpose(1, 0, 2).reshape(128, 256)
            m_fin[l * 8 + m, :, 2048:2304] = a
            b = wb[:, m * 128:(m + 1) * 128].reshape(4, 128, 128).transpose(1, 0, 2).reshape(128, 512)
            m_fin[l * 8 + m, :, 2560:3072] = b
        m_wo[l] = np.asarray(inp["w_out"][l], f32).reshape(8, 128, 1024).transpose(1, 0, 2).reshape(128, 8192)
    return f_wi, f_wo, m_wi, m_fin, m_wo


def _prep_consts(inp):
    f32 = np.float32
    cst = np.zeros((128, NCST), f32)
    norms = (inp["ffn1_norm"], inp["mix_norm"], inp["ffn2_norm"])
    for l in range(L):
        for w in range(3):
            cst[:, C_NORM + (l * 3 + w) * 8:C_NORM + (l * 3 + w) * 8 + 8] = \
                np.asarray(norms[w][l], f32).reshape(8, 128).T
        gd = np.asarray(inp["qk_gain_dil"][l], f32)
        for qk in range(2):
            for g in range(3):
                for jp in range(2):
                    cst[:, C_QKDIL + ((l * 2 + qk) * 3 + g) * 2 + jp] = gd[qk, g, 2 * jp:2 * jp + 2].reshape(128)
        gf = np.asarray(inp["qk_gain_diff"][l], f32)
        for qk in range(2):
            for h in range(4):
                cst[:, C_QKDIFF + (l * 2 + qk) * 4 + h] = gf[qk, h].reshape(128)
        cst[:, C_SUB + 4 * l:C_SUB + 4 * l + 4] = np.asarray(inp["diff_subnorm"][l], f32).T
        lam_init = 0.8 - 0.6 * math.exp(-0.3 * l)
        cst[:, C_LAMI + l] = lam_init
        cst[:, C_OML + l] = 1.0 - lam_init
        cst[:, C_LQ + l * 128:C_LQ + (l + 1) * 128] = np.asarray(inp["lambda_q"][l], f32).reshape(1, 128)
        cst[:, C_LK + l * 128:C_LK + (l + 1) * 128] = np.asarray(inp["lambda_k"][l], f32).reshape(1, 128)
    cst[:, C_EPS] = EPS
    sl = _slopes()
    p = np.arange(128, dtype=np.float64)
    for h in range(4):
        for d in range(19):
            cst[:, C_ALIBI + h * 19 + d] = (sl[12 + h] * (p + 128.0 * (d - 15))).astype(f32)
    cb = np.zeros((128, NCB), f32)
    cb[:, 0:128] = 1.0
    cb[0:64, 128:192] = 1.0
    cb[64:128, 192:256] = 1.0
    k = np.arange(128)[:, None].astype(np.float64)
    q = np.arange(128)[None, :].astype(np.float64)
    for j in range(4):
        s = 2.0 ** (-(j + 1) / 2.0)
        prev = np.where(q <= k, np.exp(-s * (128.0 + q - k)), 0.0)
        cur = np.where(q >= k, np.exp(-s * (q - k)), 0.0)
        cb[:, 256 + (2 * j) * 128:256 + (2 * j + 1) * 128] = prev
        cb[:, 256 + (2 * j + 1) * 128:256 + (2 * j + 2) * 128] = cur
    cb[:, 1280:1408] = (q >= k)
    return cst, cb


_NC_CACHE = {}


def kernel(**inputs):
    x = np.asarray(inputs["x"], np.float32)
    f_wi, f_wo, m_wi, m_fin, m_wo = _prep_weights(inputs)
    cst, cb = _prep_consts(inputs)
    if "nc" not in _NC_CACHE:
        _NC_CACHE["nc"] = build_nc()
    nc = _NC_CACHE["nc"]
    in_maps = []
    for b in range(8):
        xT = np.ascontiguousarray(x[b].T).reshape(8, 128, S)
        in_maps.append({"xT": xT, "f_wi": f_wi, "f_wo": f_wo, "m_wi": m_wi, "m_fin": m_fin,
                        "m_wo": m_wo, "cst": cst, "cb": cb})
    res = run_bass_kernel_spmd(nc, in_maps, core_ids=list(range(8)))
    out = np.empty((8, S, D), np.float32)
    for b in range(8):
        out[b] = np.asarray(res.results[b]["outT"], np.float32).reshape(D, S).T
    return out
```
